# Optimizing a Trainium2 kernel written in Bass

```python
import math
import jax, jax.numpy as jnp
from jax import lax
import numpy as np

D_MODEL = 1024
BATCH = 8
SEQ = 2048
DEPTH = 1

HEAD_DIM = 64
D_ATTN = D_MODEL // 2
D_RWKV = D_MODEL - D_ATTN
N_Q_HEADS = D_ATTN // HEAD_DIM
N_KV_HEADS = max(1, N_Q_HEADS // 4)
Q_PER_KV = N_Q_HEADS // N_KV_HEADS
D_KV = N_KV_HEADS * HEAD_DIM
WINDOW = 128
BLOCK = 128
N_BUCKETS = 32
MAX_DISTANCE = 128
N_RWKV_HEADS = D_RWKV // HEAD_DIM
LORA_DECAY = 64
LORA_ICLR = 64
LORA_GATE = 128
RWKV_COLS = 3 * D_RWKV + LORA_DECAY + LORA_ICLR + LORA_GATE
RWKV_SPLITS = (D_RWKV, 2 * D_RWKV, 3 * D_RWKV, 3 * D_RWKV + LORA_DECAY, 3 * D_RWKV + LORA_DECAY + LORA_ICLR)
D_IN = D_ATTN + 2 * D_KV + RWKV_COLS
D_FF = 4 * D_MODEL
CONV_WIDTH = 3
NORM_EPS = 1e-6
GN_EPS = 64e-5
NEG_INF = -1e30

kernel_name = "hymba_swa_sink_rwkv7_convffn_sandwich"


def rms_norm(x, g):
    xf = x.astype(jnp.float32)
    y = xf * lax.rsqrt(jnp.mean(xf * xf, axis=-1, keepdims=True) + NORM_EPS) * g.astype(jnp.float32)
    return y.astype(x.dtype)


def t5_bucket(rel):
    n = jnp.maximum(rel, 0)
    max_exact = N_BUCKETS // 2
    large = max_exact + (jnp.log(jnp.maximum(n, 1).astype(jnp.float32) / max_exact)
                         / math.log(MAX_DISTANCE / max_exact) * (N_BUCKETS - max_exact)).astype(jnp.int32)
    large = jnp.minimum(large, N_BUCKETS - 1)
    return jnp.where(n < max_exact, n, large)


def sliding_window_sink_attention(q, k, v, rel_bias, sinks):
    B, S, _ = q.shape
    NB = S // BLOCK
    q = q.reshape(B, NB, BLOCK, N_KV_HEADS, Q_PER_KV, HEAD_DIM)

    def band(t):
        t = t.reshape(B, NB, BLOCK, N_KV_HEADS, HEAD_DIM)
        prev = jnp.concatenate([jnp.zeros_like(t[:, :1]), t[:, :-1]], axis=1)
        return jnp.concatenate([prev, t], axis=2)

    kb, vb = band(k), band(v)
    rel = (jnp.arange(BLOCK)[:, None] + BLOCK) - jnp.arange(2 * BLOCK)[None, :]
    in_window = (rel >= 0) & (rel < WINDOW)
    key_pos = (jnp.arange(NB)[:, None] - 1) * BLOCK + jnp.arange(2 * BLOCK)[None, :]
    mask = in_window[None] & (key_pos >= 0)[:, None, :]
    bias = rel_bias.astype(jnp.float32)[t5_bucket(rel)]
    bias = bias.transpose(2, 0, 1).reshape(N_KV_HEADS, Q_PER_KV, BLOCK, 2 * BLOCK)
    scores = jnp.einsum('bnqhgd,bnkhd->bnhgqk', q, kb).astype(jnp.float32) * (HEAD_DIM ** -0.5) + bias
    scores = jnp.where(mask[None, :, None, None], scores, NEG_INF)
    sink = sinks.astype(jnp.float32).reshape(N_KV_HEADS, Q_PER_KV)[:, :, None, None]
    m = jnp.maximum(scores.max(axis=-1, keepdims=True), sink)
    p = jnp.exp(scores - m)
    probs = p / (p.sum(axis=-1, keepdims=True) + jnp.exp(sink - m))
    out = jnp.einsum('bnhgqk,bnkhd->bnqhgd', probs.astype(v.dtype), vb)
    return out.reshape(B, S, D_ATTN)


def rwkv7_time_mix(p, w0, w_decay_up, a0, w_iclr_up, w_gate_up, k_k, k_a, r_k, ln_x_g, ln_x_b):
    B, S, _ = p.shape
    H, N = N_RWKV_HEADS, HEAD_DIM
    p = p.astype(jnp.float32)
    r, k, v, zw, za, zg = jnp.split(p, RWKV_SPLITS, axis=-1)
    w_log = -jax.nn.softplus(-(w0 + jnp.tanh(zw) @ w_decay_up)) - 0.5
    decay = jnp.exp(-jnp.exp(w_log))
    a = jax.nn.sigmoid(a0 + za @ w_iclr_up)
    g = jax.nn.sigmoid(zg) @ w_gate_up
    kk = (k * k_k).reshape(B, S, H, N)
    kk = kk / jnp.maximum(jnp.sqrt(jnp.sum(kk * kk, axis=-1, keepdims=True)), 1e-12)
    k = k * (1.0 + (a - 1.0) * k_a)

    def heads(t):
        return t.reshape(B, S, H, N)

    def tmaj(t):
        return t.swapaxes(0, 1)

    rh, kh, vh = heads(r), heads(k), heads(v)

    def step(state, inp):
        r_t, w_t, k_t, v_t, kk_t, a_t = inp
        sa = jnp.einsum('bhvk,bhk->bhv', state, -kk_t)
        state = (state * w_t[:, :, None, :]
                 + sa[..., None] * (kk_t * a_t)[:, :, None, :]
                 + v_t[..., None] * k_t[:, :, None, :])
        return state, jnp.einsum('bhvk,bhk->bhv', state, r_t)

    state0 = jnp.zeros((B, H, N, N), jnp.float32)
    _, o = lax.scan(step, state0, (tmaj(rh), tmaj(heads(decay)), tmaj(kh), tmaj(vh), tmaj(kk), tmaj(heads(a))))
    o = o.swapaxes(0, 1)
    mu = jnp.mean(o, axis=-1, keepdims=True)
    var = jnp.mean(jnp.square(o - mu), axis=-1, keepdims=True)
    o = ((o - mu) * lax.rsqrt(var + GN_EPS)).reshape(B, S, D_RWKV) * ln_x_g + ln_x_b
    bonus = jnp.sum(rh * kh * r_k, axis=-1, keepdims=True) * vh
    o = o + bonus.reshape(B, S, D_RWKV)
    return o * g


def conv_gated_ffn(h, w_up, conv_w, conv_b, w_down):
    S = h.shape[1]
    u = h @ w_up
    u_pad = jnp.pad(u, ((0, 0), (CONV_WIDTH - 1, 0), (0, 0)))
    u = conv_b + sum(conv_w[j] * u_pad[:, j:j + S] for j in range(CONV_WIDTH))
    gate, val = jnp.split(u, 2, axis=-1)
    return (jax.nn.gelu(gate, approximate=True) * val) @ w_down


def setup_inputs(seed: int = 0) -> dict:
    key = jax.random.key(seed)
    ks = jax.random.split(key, 24)
    L = DEPTH
    nrm = lambda k, shape, s: jax.random.normal(k, shape, jnp.float32) * s
    return {
        "x": jax.random.normal(ks[0], (BATCH, SEQ, D_MODEL), jnp.float32),
        "norm_mix_pre": 1.0 + nrm(ks[1], (L, D_MODEL), 0.02),
        "norm_mix_post": 1.0 + nrm(ks[2], (L, D_MODEL), 0.02),
        "norm_ffn_pre": 1.0 + nrm(ks[3], (L, D_MODEL), 0.02),
        "norm_ffn_post": 1.0 + nrm(ks[4], (L, D_MODEL), 0.02),
        "w_in": nrm(ks[5], (L, D_MODEL, D_IN), D_MODEL ** -0.5),
        "rel_bias": nrm(ks[6], (N_BUCKETS, N_Q_HEADS), 0.5),
        "sinks": nrm(ks[7], (L, N_Q_HEADS), 0.5),
        "rwkv_shift_mix": jax.random.uniform(ks[8], (L, RWKV_COLS), jnp.float32),
        "w0": jax.random.uniform(ks[9], (L, D_RWKV), jnp.float32, -5.0, 0.0),
        "w_decay_up": nrm(ks[10], (L, LORA_DECAY, D_RWKV), 0.5 * LORA_DECAY ** -0.5),
        "a0": nrm(ks[11], (L, D_RWKV), 0.1),
        "w_iclr_up": nrm(ks[12], (L, LORA_ICLR, D_RWKV), 0.5 * LORA_ICLR ** -0.5),
        "w_gate_up": nrm(ks[13], (L, LORA_GATE, D_RWKV), LORA_GATE ** -0.5),
        "k_k": 0.85 + nrm(ks[14], (L, D_RWKV), 0.05),
        "k_a": 1.0 + nrm(ks[15], (L, D_RWKV), 0.05),
        "r_k": nrm(ks[16], (L, N_RWKV_HEADS, HEAD_DIM), 0.1),
        "ln_x_g": 1.0 + nrm(ks[17], (L, D_RWKV), 0.02),
        "ln_x_b": nrm(ks[18], (L, D_RWKV), 0.02),
        "w_out": nrm(ks[19], (L, D_MODEL, D_MODEL), D_MODEL ** -0.5),
        "w_ffn_up": nrm(ks[20], (L, D_MODEL, 2 * D_FF), D_MODEL ** -0.5),
        "conv_w": nrm(ks[21], (L, CONV_WIDTH, 2 * D_FF), CONV_WIDTH ** -0.5),
        "conv_b": nrm(ks[22], (L, 2 * D_FF), 0.02),
        "w_ffn_down": nrm(ks[23], (L, D_FF, D_MODEL), D_FF ** -0.5),
    }


def reference(x, norm_mix_pre, norm_mix_post, norm_ffn_pre, norm_ffn_post, w_in, rel_bias, sinks,
              rwkv_shift_mix, w0, w_decay_up, a0, w_iclr_up, w_gate_up, k_k, k_a, r_k, ln_x_g, ln_x_b,
              w_out, w_ffn_up, conv_w, conv_b, w_ffn_down):
    for l in range(DEPTH):
        h = rms_norm(x, norm_mix_pre[l])
        proj = h @ w_in[l]
        q, k, v, p = jnp.split(proj, (D_ATTN, D_ATTN + D_KV, D_ATTN + 2 * D_KV), axis=-1)
        attn = sliding_window_sink_attention(q, k, v, rel_bias, sinks[l])
        p_prev = jnp.concatenate([jnp.zeros_like(p[:, :1]), p[:, :-1]], axis=1)
        p = p + (p_prev - p) * rwkv_shift_mix[l]
        rw = rwkv7_time_mix(p, w0[l], w_decay_up[l], a0[l], w_iclr_up[l], w_gate_up[l],
                            k_k[l], k_a[l], r_k[l], ln_x_g[l], ln_x_b[l])
        mix = jnp.concatenate([attn, rw.astype(x.dtype)], axis=-1) @ w_out[l]
        x = x + rms_norm(mix, norm_mix_post[l])
        f = conv_gated_ffn(rms_norm(x, norm_ffn_pre[l]), w_ffn_up[l], conv_w[l], conv_b[l], w_ffn_down[l])
        x = x + rms_norm(f, norm_ffn_post[l])
    return x
```

```python
import math
import os
import numpy as np
import concourse.bass as bass
import concourse.mybir as mybir
from concourse.bass_utils import run_bass_kernel_spmd

F32 = mybir.dt.float32
BF16 = mybir.dt.bfloat16
U8 = mybir.dt.uint8
AF = mybir.ActivationFunctionType
ALU = mybir.AluOpType
AX = mybir.AxisListType

S = 2048
D = 1024
NB = 16
DSZ = {F32: 4, BF16: 2, U8: 1}
SOFT_RELEASE = os.environ.get("K_SOFT", "1") == "1"
EMBED_WAIT = tuple(x for x in os.environ.get("K_EMBED", "pe,act,dve,pool").split(",") if x)


class Buf:
    __slots__ = ("ap", "last_w", "readers", "name", "excl")

    def __init__(self, ap, name="", excl=False):
        self.ap = ap
        self.excl = excl
        self.last_w = []
        self.readers = {}
        self.name = name

    def __getitem__(self, k):
        return self.ap[k]


class Eng:
    def __init__(self, name, h, sem):
        self.name = name
        self.h = h
        self.sem = sem
        self.count = 0
        self.seen = {}
        self.log = []
        self.pend = []


class K:
    def __init__(self, nc):
        self.nc = nc
        self.sems = []
        self.pe = self._eng("pe", nc.tensor)
        self.act = self._eng("act", nc.scalar)
        self.dve = self._eng("dve", nc.vector)
        self.pool = self._eng("pool", nc.gpsimd)
        self.sp = self._eng("sp", nc.sync)
        self.engs = [self.pe, self.act, self.dve, self.pool, self.sp]
        self.dma_sems = []
        self.rr = 0
        self.arena = None
        self.top = 0
        self.cap = 0
        self.ghosts = []
        self.live = []

    def _eng(self, name, h):
        sem = self.nc.alloc_semaphore(name="S_" + name)
        return Eng(name, h, sem)

    def set_arena(self, tensor, cap):
        self.arena = tensor
        self.cap = cap
        self.top = 0
        self.htop = cap

    def release_high(self, n_bytes_top, hard=False):
        if hard or not SOFT_RELEASE:
            self.barrier()
        self._ghost(self.htop, n_bytes_top)
        self.htop = n_bytes_top

    def alloc(self, name, shape, dtype, high=False):
        n = int(np.prod(shape)) * DSZ[dtype]
        n = (n + 63) // 64 * 64
        if high:
            self.htop -= n
            off = self.htop
            assert self.top <= self.htop, f"SBUF arena overflow allocating {name} (high)"
        else:
            off = self.top
            assert off + n <= self.htop, f"SBUF arena overflow allocating {name}: {off}+{n}>{self.htop}"
            self.top += n
        inherit = {}
        for (g0, g1, toks) in self.ghosts:
            if g0 < off + n and off < g1:
                for key, t in toks.items():
                    if key not in inherit or inherit[key][1] < t[1]:
                        inherit[key] = t
        self.ghosts = [g for g in self.ghosts if not (off <= g[0] and g[1] <= off + n)]
        self._pending_inherit = inherit
        self._pending_range = (off, off + n)
        ap = self.arena[:, off:off + n]
        if dtype != U8:
            ap = ap.bitcast(dtype)
        nel = int(np.prod(shape))
        ap = ap[:, 0:nel]
        if len(shape) == 2:
            ap = ap.rearrange("p (a b) -> p a b", b=shape[1])
        elif len(shape) == 3:
            ap = ap.rearrange("p (a b c) -> p a b c", b=shape[1], c=shape[2])
        elif len(shape) == 4:
            ap = ap.rearrange("p (a b c d) -> p a b c d", b=shape[1], c=shape[2], d=shape[3])
        b = Buf(ap, name)
        b.readers = dict(self._pending_inherit)
        self.live.append((self._pending_range[0], self._pending_range[1], b))
        return b

    def mark(self):
        return self.top

    @staticmethod
    def _toks_of(b):
        toks = {}
        for t in list(b.last_w) + list(b.readers.values()):
            if t[0].num not in toks or toks[t[0].num][1] < t[1]:
                toks[t[0].num] = t
        return toks

    def adopt(self, parent, children):
        for c in children:
            for key, t in self._toks_of(c).items():
                if key not in parent.readers or parent.readers[key][1] < t[1]:
                    parent.readers[key] = t

    def _ghost(self, lo, hi):
        keep = []
        for (a0, a1, b) in self.live:
            if a0 >= lo and a1 <= hi:
                self.ghosts.append((a0, a1, self._toks_of(b)))
            else:
                keep.append((a0, a1, b))
        self.live = keep

    def release(self, mark, hard=False):
        if hard or not SOFT_RELEASE:
            self.barrier()
        self._ghost(mark, self.top)
        self.top = mark

    def dma_sem(self, name):
        sem = self.nc.alloc_semaphore(name="D_" + name)
        ent = [sem, 0]
        self.dma_sems.append(ent)
        return ent

    def _wait(self, eng, toks):
        need = {}
        for (sem, val) in toks:
            key = sem.num
            for ent in self.dma_sems:
                if ent[0].num == key:
                    val = max(val, ent[1])
                    break
            if key not in need or need[key][1] < val:
                need[key] = (sem, val)
        for key, (sem, val) in need.items():
            if eng.seen.get(key, 0) < val:
                eng.h.wait_ge(sem, val)
                eng.seen[key] = val
                eng.pend.append((key, val))

    def _deps(self, eng, reads, writes):
        toks = []
        for b in reads:
            toks.extend(b.last_w)
        for b in writes:
            for t in b.last_w:
                if t[0].num != eng.sem.num:
                    toks.append(t)
            for key, t in b.readers.items():
                if key != eng.sem.num:
                    toks.append(t)
        return toks

    def op(self, eng, fn, reads=(), writes=(), inc=True):
        ex = [b for b in reads if b.excl]
        if ex:
            reads = [b for b in reads if not b.excl]
            writes = list(writes) + [b for b in ex if b not in writes]
        emb = None
        if EMBED_WAIT and eng.name in EMBED_WAIT:
            need = {}
            for (sem, val) in self._deps(eng, reads, writes):
                for ent in self.dma_sems:
                    if ent[0].num == sem.num:
                        val = max(val, ent[1])
                        break
                if eng.seen.get(sem.num, 0) < val and (sem.num not in need or need[sem.num][1] < val):
                    need[sem.num] = (sem, val)
            if need:
                keys = list(need.keys())
                emb = need[keys[-1]]
                rest = [need[k_] for k_ in keys[:-1]]
                self._wait(eng, rest)
                eng.seen[emb[0].num] = emb[1]
                eng.pend.append((emb[0].num, emb[1]))
        else:
            self._wait(eng, self._deps(eng, reads, writes))
        ins = fn()
        if emb is not None:
            ins._wait_ge(emb[0], emb[1])
        if inc:
            eng.count += 1
            ins.then_inc(eng.sem, 1)
            tok = (eng.sem, eng.count)
            eng.log.append((eng.pend, [(eng.sem.num, 1)]))
        else:
            tok = (eng.sem, eng.count + 1)
            eng.log.append((eng.pend, []))
        eng.pend = []
        for b in writes:
            b.last_w = [tok]
            b.readers = {}
        for b in reads:
            if b in writes:
                continue
            b.readers[eng.sem.num] = tok
        return ins

    def dma(self, eng, out, in_, reads=(), writes=(), sem=None):
        self._wait(eng, self._deps(eng, reads, writes))
        if sem is None:
            sem = self.dma_sems[0]
        ins = eng.h.dma_start(out=out, in_=in_)
        sem[1] += 16
        ins.then_inc(sem[0], 16)
        eng.log.append((eng.pend, [(sem[0].num, 16)]))
        eng.pend = []
        tok = (sem[0], sem[1])
        for b in writes:
            b.last_w = [tok]
            b.readers = {}
        for b in reads:
            b.readers[sem[0].num] = tok
        return ins

    def barrier(self):
        toks = [(e.sem, e.count) for e in self.engs if e.count > 0]
        toks += [(s[0], s[1]) for s in self.dma_sems if s[1] > 0]
        for e in self.engs:
            self._wait(e, [t for t in toks if t[0].num != e.sem.num])
            if e.pend:
                e.log.append((e.pend, []))
                e.pend = []

    def simulate(self):
        sem = {}
        pc = {e.name: 0 for e in self.engs}
        progress = True
        while progress:
            progress = False
            for e in self.engs:
                while pc[e.name] < len(e.log):
                    waits, incs = e.log[pc[e.name]]
                    if all(sem.get(k_, 0) >= v for k_, v in waits):
                        for k_, a in incs:
                            sem[k_] = sem.get(k_, 0) + a
                        pc[e.name] += 1
                        progress = True
                    else:
                        break
        stuck = {e.name: (pc[e.name], len(e.log)) for e in self.engs if pc[e.name] < len(e.log)}
        if stuck:
            info = {}
            for e in self.engs:
                if pc[e.name] < len(e.log):
                    waits, _ = e.log[pc[e.name]]
                    info[e.name] = [(k_, v, sem.get(k_, 0)) for k_, v in waits if sem.get(k_, 0) < v]
            raise RuntimeError(f"DEADLOCK in sync graph: {stuck} {info} sems={ {e.name: e.sem.num for e in self.engs} }")
        return True

    def ev(self):
        mode = os.environ.get("K_EV", "alt")
        if mode == "act":
            return self.act
        if mode == "dve":
            return self.dve
        self.rr ^= 1
        return self.act if self.rr else self.dve


def copy_on(k, eng, out, in_, reads, writes, scale=None):
    if eng is k.act:
        if scale is None:
            return k.op(eng, lambda: eng.h.activation(out=out, in_=in_, func=AF.Copy), reads, writes)
        return k.op(eng, lambda: eng.h.activation(out=out, in_=in_, func=AF.Copy, scale=scale), reads, writes)
    if scale is None:
        return k.op(eng, lambda: eng.h.tensor_copy(out, in_), reads, writes)
    return k.op(eng, lambda: eng.h.tensor_scalar(out, in_, scale, None, ALU.mult), reads, writes)


def t5_bucket_np(rel):
    n = np.maximum(rel, 0)
    me = 16
    large = me + (np.log(np.maximum(n, 1).astype(np.float32) / me) / math.log(128 / me) * (32 - me)).astype(np.int32)
    large = np.minimum(large, 31)
    return np.where(n < me, n, large)


def w_in_perm():
    cols = []
    for j in range(4):
        cols += list(range(j * 64, j * 64 + 64)) + list(range(256 + j * 64, 256 + j * 64 + 64))
    cols += list(range(512, 768))
    P0 = 768
    cols += list(range(P0 + 1536, P0 + 1664))
    cols += list(range(P0 + 1664, P0 + 1792))
    for hp in range(4):
        for q in range(3):
            cols += list(range(P0 + q * 512 + hp * 128, P0 + q * 512 + hp * 128 + 128))
    return np.array(cols)


def host_consts():
    c = {}
    c["ident"] = np.eye(128, dtype=np.float32)
    s = np.arange(128)[:, None]
    t = np.arange(128)[None, :]
    su = (s < t).astype(np.float32)
    iu = (s <= t).astype(np.float32)
    sl = (s > t).astype(np.float32)
    c["rmask"] = np.concatenate([su, iu, sl], axis=1)
    kk = np.arange(128)[:, None]
    qq = np.arange(128)[None, :]
    m = np.zeros((128, 2, 128), np.float32)
    m[:, 0, :] = np.where(kk > qq, 0.0, -1e4)
    m[:, 1, :] = np.where(kk <= qq, 0.0, -1e4)
    c["amask"] = m
    rel = np.zeros((128, 2, 128), np.int64)
    rel[:, 0, :] = qq + 128 - kk
    rel[:, 1, :] = qq - kk
    c["_bucket"] = t5_bucket_np(rel)
    ind = np.zeros((128, 2), np.float32)
    ind[:64, 0] = 1.0
    ind[64:, 1] = 1.0
    c["ind2"] = ind
    blk = np.zeros((128, 128), np.float32)
    blk[:64, :64] = 1.0
    blk[64:, 64:] = 1.0
    c["blk"] = blk
    return c


def host_layout(inp):
    l = 0
    f = lambda a: np.ascontiguousarray(np.asarray(a, dtype=np.float32))
    hc = host_consts()
    d = {}
    perm = w_in_perm()
    d["w_in"] = f(inp["w_in"][l][:, perm])
    mix = np.asarray(inp["rwkv_shift_mix"][l], np.float32)
    mixp = mix[perm[768:] - 768]
    d["mixc"] = f(mixp.reshape(14, 128).T)
    d["g_pre"] = f(np.asarray(inp["norm_mix_pre"][l]).reshape(8, 128).T)
    d["g_ffn"] = f(np.asarray(inp["norm_ffn_pre"][l]).reshape(8, 128).T)
    d["g_post_b"] = f(np.broadcast_to(np.asarray(inp["norm_mix_post"][l])[None, :], (128, 1024)))
    d["g_fpost_b"] = f(np.broadcast_to(np.asarray(inp["norm_ffn_post"][l])[None, :], (128, 1024)))
    d["ident"] = hc["ident"]
    d["rmask"] = hc["rmask"]
    d["amask"] = hc["amask"]
    d["ind2"] = hc["ind2"]
    d["blk"] = hc["blk"]
    rb = np.asarray(inp["rel_bias"], np.float32)
    d["abias"] = f(rb[hc["_bucket"]].transpose(0, 1, 3, 2))
    d["sinks"] = f(np.asarray(inp["sinks"][l]).reshape(1, 8))
    col = lambda name: f(np.asarray(inp[name][l]).reshape(4, 128).T)
    d["rw_cols"] = f(np.concatenate([col("w0"), col("a0"), col("k_k"), col("k_a"),
                                     np.asarray(inp["r_k"][l], np.float32).reshape(4, 128).T], axis=1))
    d["lnx_b"] = f(np.stack([np.broadcast_to(np.asarray(inp["ln_x_g"][l])[None, :], (128, 512)),
                             np.broadcast_to(np.asarray(inp["ln_x_b"][l])[None, :], (128, 512))], axis=1))
    d["w_lora"] = f(np.concatenate([inp["w_decay_up"][l], inp["w_iclr_up"][l]], axis=0))
    d["w_gate"] = f(inp["w_gate_up"][l])
    d["w_out"] = f(inp["w_out"][l])
    wu = np.asarray(inp["w_ffn_up"][l], np.float32)
    d["w_up"] = f(wu.reshape(1024, 2, 32, 128).transpose(0, 2, 1, 3).reshape(1024, 8192))
    cw = np.asarray(inp["conv_w"][l], np.float32).reshape(3, 2, 32, 128)
    d["conv_w"] = f(cw.transpose(3, 2, 1, 0).reshape(128, 64, 3))
    cb = np.asarray(inp["conv_b"][l], np.float32).reshape(2, 32, 128)
    d["conv_b"] = f(cb.transpose(2, 1, 0).reshape(128, 64))
    d["w_down"] = f(inp["w_ffn_down"][l])
    return d


SHARED = ["w_in", "mixc", "g_pre", "g_ffn", "g_post_b", "g_fpost_b", "ident", "rmask", "amask", "ind2", "blk",
          "abias", "sinks", "rw_cols", "lnx_b", "w_lora", "w_gate", "w_out", "w_up", "conv_w", "conv_b", "w_down"]
SHAPES = {"w_in": [1024, 2560], "mixc": [128, 14], "g_pre": [128, 8], "g_ffn": [128, 8], "g_post_b": [128, 1024],
          "g_fpost_b": [128, 1024], "ident": [128, 128], "rmask": [128, 384], "amask": [128, 2, 128], "ind2": [128, 2],
          "blk": [128, 128], "abias": [128, 2, 8, 128], "sinks": [1, 8], "rw_cols": [128, 20], "lnx_b": [128, 2, 512],
          "w_lora": [128, 512], "w_gate": [128, 512], "w_out": [1024, 1024], "w_up": [1024, 8192],
          "conv_w": [128, 64, 3], "conv_b": [128, 64], "w_down": [4096, 1024]}

ARENA_BYTES = 206 * 1024


class Prog:
    def __init__(self, debug=()):
        self.debug = set(debug)
        nc = bass.Bass("TRN2", target_bir_lowering=False)
        self.nc = nc
        self.din = {n: nc.dram_tensor(n, SHAPES[n], F32, kind="ExternalInput").ap() for n in SHARED}
        self.x = nc.dram_tensor("x", [S, D], F32, kind="ExternalInput").ap()
        self.out = nc.dram_tensor("out", [S, D], F32, kind="ExternalOutput").ap()
        self.dbg = {}
        self.early_consts = False
        self.k = K(nc)

    def dbg_out(self, name, shape):
        t = self.nc.dram_tensor("dbg_" + name, shape, F32, kind="ExternalOutput").ap()
        self.dbg[name] = t
        return t

    def build(self, upto="all"):
        nc, k = self.nc, self.k
        self.upto = upto
        with nc.sbuf_tensor("arena", [128, ARENA_BYTES], U8) as arena, \
                nc.psum_tensor("ps0", [128, 512], F32) as p0, nc.psum_tensor("ps1", [128, 512], F32) as p1, \
                nc.psum_tensor("ps2", [128, 512], F32) as p2, nc.psum_tensor("ps3", [128, 512], F32) as p3, \
                nc.psum_tensor("ps4", [128, 512], F32) as p4, nc.psum_tensor("ps5", [128, 512], F32) as p5, \
                nc.psum_tensor("ps6", [128, 512], F32) as p6, nc.psum_tensor("ps7", [128, 512], F32) as p7:
            k.set_arena(arena, ARENA_BYTES)
            self.ps = [Buf(p[:, :], f"ps{i}", excl=True) for i, p in enumerate([p0, p1, p2, p3, p4, p5, p6, p7])]
            self.psi = 0
            self.sem_c = k.dma_sem("const")
            self.sem_o = k.dma_sem("out")
            self.sem_o2 = [self.sem_o, k.dma_sem("out1")]
            self.sem_dbg = k.dma_sem("dbg")
            self.early_consts = True
            self.consts()
            self.p3_consts()
            self.early_consts = False
            self.phase1()
            if upto == "p1":
                return self.finish()
            self.phase2()
            if upto == "p2":
                return self.finish()
            self.phase3()
            if upto.startswith("p3"):
                return self.finish()
            self.phase5()
            if upto == "p5":
                return self.finish()
            self.phase6()
            self.finish()

    def finish(self):
        self.k.barrier()
        self.k.simulate()
        return self.nc

    def bank(self):
        b = self.ps[self.psi]
        self.psi = (self.psi + 1) % 8
        return b

    def load_const(self, name, shape, dtype=F32, src=None, eng=None):
        k = self.k
        b = k.alloc(name, shape, dtype)
        src = self.din[name] if src is None else src
        eng = eng or (k.pool if dtype != F32 else (k.act if self.early_consts else k.sp))
        k.dma(eng, b[:], src, writes=[b], sem=self.sem_c)
        return b

    def consts(self):
        k, nc = self.k, self.nc
        self.identf = self.load_const("ident", [128])
        self.identb = self.load_const("identb", [128], BF16, src=self.din["ident"])
        self.g_pre = self.load_const("g_pre", [8])
        self.g_ffn = self.load_const("g_ffn", [8])
        self.eps = k.alloc("eps", [1], F32)
        k.op(k.pool, lambda: nc.gpsimd.memset(self.eps[:], 1e-6), [], [self.eps])
        self.zero1 = k.alloc("zero1", [1], F32)
        k.op(k.pool, lambda: nc.gpsimd.memset(self.zero1[:], 0.0), [], [self.zero1])
        self.A_tok = k.alloc("A_tok", [NB, 512], BF16, high=True)
        self.R_tok = k.alloc("R_tok", [NB, 512], BF16, high=True)
        self.h_mark = k.htop
        self.hT = k.alloc("hT", [8, S], BF16, high=True)
        self.h_mark_hT = k.htop
        self.sem_wpre = k.dma_sem("wpre")
        w_src = self.din["w_in"].rearrange("(kc p) n -> p kc n", p=128)
        self.wl = k.alloc("wl", [8, 256], BF16, high=True)
        self.h_mark_wl = k.htop
        self.wq = k.alloc("wq", [8, 768], BF16, high=True)
        self._wsrc = w_src

    def prefetch_qkv(self, after_tokens):
        k = self.k
        k._wait(k.pool, after_tokens)
        w_src = self._wsrc
        for h4 in range(2):
            k.dma(k.pool, self.wq[:, h4 * 4:(h4 + 1) * 4, :], w_src[:, h4 * 4:(h4 + 1) * 4, 0:768], writes=[self.wq], sem=self.sem_wpre)
        for h4 in range(2):
            k.dma(k.pool, self.wl[:, h4 * 4:(h4 + 1) * 4, :], w_src[:, h4 * 4:(h4 + 1) * 4, 768:1024], writes=[self.wl], sem=self.sem_wpre)

    def norm_T(self, src_fn, g_cols, dst, ngroups, xt_ring, xn_ring, sq, ss, rs, t_base=0):
        for _ in self.norm_T_gen(src_fn, g_cols, dst, ngroups, xt_ring, xn_ring, sq, ss, rs):
            pass

    def norm_T_gen(self, src_fn, g_cols, dst, ngroups, xt_ring, xn_ring, sq, ss, rs, G=None, pipelined=False, act_kc=None):
        k, nc = self.k, self.nc
        G = G or len(xn_ring)
        NX = len(xn_ring)

        def part_a(grp):
            for i in range(G):
                t = grp * G + i
                xt = xt_ring[t % len(xt_ring)]
                src_fn(t, xt)
                k.op(k.act, lambda: nc.scalar.activation(out=sq[:], in_=xt[:], func=AF.Square, accum_out=ss[:, t:t + 1]),
                     [xt], [sq, ss])
            sl = slice(grp * G, grp * G + G)
            k.op(k.dve, lambda: nc.vector.tensor_scalar(rs[:, sl], ss[:, sl], 1.0 / D, 1e-6, ALU.mult, ALU.add), [ss], [rs])
            k.op(k.act, lambda: nc.scalar.activation(out=rs[:, sl], in_=rs[:, sl], func=AF.Sqrt), [rs], [rs])
            k.op(k.dve, lambda: nc.vector.reciprocal(rs[:, sl], rs[:, sl]), [rs], [rs])
            for i in range(G):
                t = grp * G + i
                xt = xt_ring[t % len(xt_ring)]
                xn = xn_ring[t % NX]
                k.op(k.dve, lambda: nc.vector.tensor_scalar(xn[:], xt[:], rs[:, t:t + 1], None, ALU.mult), [xt, rs], [xn])

        def part_b(grp):
            for kc in range(8):
                bk = self.bank()
                for i in range(G):
                    xn = xn_ring[(grp * G + i) % NX]
                    k.op(k.pe, lambda: nc.tensor.transpose(out=bk[:, i * 128:(i + 1) * 128], in_=xn[:, kc * 128:(kc + 1) * 128],
                                                           identity=self.identf[:]), [xn, self.identf], [bk], inc=(i == G - 1))
                ee = k.ev() if act_kc is None else (k.act if kc in act_kc else k.dve)
                copy_on(k, ee, dst[:, kc, grp * G * 128:(grp + 1) * G * 128], bk[:, 0:G * 128], [bk, g_cols], [dst], scale=g_cols[:, kc:kc + 1])

        if not pipelined:
            for grp in range(ngroups):
                part_a(grp)
                yield "A"
                part_b(grp)
                yield "B"
        else:
            part_a(0)
            for grp in range(ngroups):
                if grp + 1 < ngroups:
                    part_a(grp + 1)
                    yield "A"
                part_b(grp)
                yield "B"

    def phase1(self):
        k, nc = self.k, self.nc
        mk = k.mark()
        xt_ring = [k.alloc(f"xt{i}", [D], F32) for i in range(8)]
        xn_ring = [k.alloc(f"xn{i}", [D], F32) for i in range(8)]
        sq = k.alloc("sq", [D], F32)
        ss = k.alloc("ss", [NB], F32)
        rs = k.alloc("rs", [NB], F32)
        sems = [k.dma_sem(f"x{i}") for i in range(8)]

        def src(t, xt):
            k.dma(k.sp, xt[:], self.x[t * 128:(t + 1) * 128, :], writes=[xt], sem=sems[t % 8])
            if t == 7:
                self.prefetch_qkv(list(xt.last_w))
        akc = {"2": (3, 7), "0": (), "4": None}[os.environ.get("K_P1A", "0")]
        for _ in self.norm_T_gen(src, self.g_pre, self.hT, 4, xt_ring, xn_ring, sq, ss, rs, G=4, pipelined=True, act_kc=akc):
            pass
        self.x_sems = sems
        if "hT" in self.debug:
            self.dump("hT", self.hT, [8, S])
        k.release(mk)

    def dump(self, name, buf, shape, part=128):
        k, nc = self.k, self.nc
        n = int(np.prod(shape))
        o = self.dbg_out(name, [128, n])
        mk = k.mark()
        CH = 2048
        st = k.alloc("dbg_st", [CH], F32)
        flat = buf.ap
        if len(shape) == 2:
            flat = flat.rearrange("p a b -> p (a b)")
        elif len(shape) == 3:
            flat = flat.rearrange("p a b c -> p (a b c)")
        elif len(shape) == 4:
            flat = flat.rearrange("p a b c d -> p (a b c d)")
        for c0 in range(0, n, CH):
            w = min(CH, n - c0)
            k.op(k.dve, lambda: nc.vector.tensor_copy(st[:, 0:w], flat[:, c0:c0 + w]), [buf], [st])
            k.dma(k.sp, o[:, c0:c0 + w], st[:, 0:w], reads=[st], sem=self.sem_dbg)
        k.release(mk)

    def proj_fm(self, w, wcols, dst_fn, ntq=4, M=128, tq_list=None):
        k, nc = self.k, self.nc
        for tq in (tq_list if tq_list is not None else range(ntq)):
            bk = self.bank()
            for kc in range(8):
                k.op(k.pe, lambda: nc.tensor.matmul(bk[0:M, :], w[:, kc, wcols], self.hT[:, kc, tq * 512:(tq + 1) * 512],
                                                    start=(kc == 0), stop=(kc == 7)),
                     [w, self.hT], [bk], inc=(kc == 7))
            dst_fn(tq, bk)

    def phase2(self):
        k, nc = self.k, self.nc
        mk = k.mark()
        wq = self.wq
        QT = k.alloc("QT", [4, S], BF16)
        KT = k.alloc("KT", [S], BF16)
        Vx = k.alloc("Vx", [NB, 2, 66], BF16)
        k.op(k.pool, lambda: nc.gpsimd.memset(Vx[:], 1.0), [], [Vx])
        biasm = self.load_const("abias", [2, 8, 128])
        amask = self.load_const("amask", [2, 128])
        sinks = self.load_const("sinks", [8], src=self.din["sinks"].partition_broadcast(128))
        for w_ in range(2):
            k.op(k.dve, lambda: nc.vector.tensor_tensor(out=biasm[:, w_], in0=biasm[:, w_],
                                                        in1=amask[:, w_:w_ + 1, :].to_broadcast([128, 8, 128]), op=ALU.add),
                 [biasm, amask], [biasm])
            k.op(k.dve, lambda: nc.vector.tensor_tensor(out=biasm[:, w_], in0=biasm[:, w_],
                                                        in1=sinks[:, :].unsqueeze(2).to_broadcast([128, 8, 128]), op=ALU.subtract),
                 [biasm, sinks], [biasm])
        for j in range(4):
            self.proj_fm(wq, slice(j * 128, (j + 1) * 128),
                         lambda tq, bk: copy_on(k, k.ev(), QT[:, j, tq * 512:(tq + 1) * 512], bk[:, :], [bk], [QT]))
        self.proj_fm(wq, slice(512, 640), lambda tq, bk: copy_on(k, k.ev(), KT[:, tq * 512:(tq + 1) * 512], bk[:, :], [bk], [KT]))
        for n in range(NB):
            bk = self.bank()
            for kc in range(8):
                k.op(k.pe, lambda: nc.tensor.matmul(bk[:, 0:128], self.hT[:, kc, n * 128:(n + 1) * 128], wq[:, kc, 640:768],
                                                    start=(kc == 0), stop=(kc == 7)), [wq, self.hT], [bk], inc=(kc == 7))
            copy_on(k, k.ev(), Vx[:, n, :, 0:64], bk[:, 0:128].rearrange("p (h d) -> p h d", d=64), [bk], [Vx])
        if "QT" in self.debug:
            self.dump("QT", QT, [4, S])
            self.dump("KT", KT, [S])
            self.dump("Vx", Vx, [NB, 2, 66])
        sp_ring = [k.alloc(f"sp{i}", [512], F32) for i in range(4)]
        pt_ring = [k.alloc(f"pt{i}", [4, 128], BF16) for i in range(4)]
        den = [k.alloc(f"den{i}", [4], F32) for i in range(2)]
        st_ = {"ri": 0}

        def att_a(n, ph):
            rows = slice(ph * 64, ph * 64 + 64)
            pts = {}
            for w_ in ((1,) if n == 0 else (0, 1)):
                kb = n - 1 + w_
                bk = self.bank()
                for j in range(4):
                    k.op(k.pe, lambda: nc.tensor.matmul(bk[:, j * 128:(j + 1) * 128], KT[rows, kb * 128:(kb + 1) * 128],
                                                        QT[rows, j, n * 128:(n + 1) * 128], start=True, stop=True),
                         [KT, QT], [bk], inc=(j == 3))
                sp = sp_ring[st_["ri"] % 4]
                pt = pt_ring[st_["ri"] % 4]
                st_["ri"] += 1
                k.op(k.dve, lambda: nc.vector.scalar_tensor_tensor(
                    out=sp[:], in0=bk[:, :], scalar=0.125,
                    in1=biasm[:, w_, 4 * ph:4 * ph + 4, :].rearrange("p h q -> p (h q)"), op0=ALU.mult, op1=ALU.add),
                    [bk, biasm], [sp])
                k.op(k.act, lambda: nc.scalar.activation(out=pt[:].rearrange("p h q -> p (h q)"), in_=sp[:], func=AF.Exp),
                     [sp], [pt])
                pts[w_] = (pt, kb)
            return pts

        def att_b(n, ph, pts):
            bo = self.bank()
            bov = bo[:, 0:264].rearrange("p (h d) -> p h d", d=66)
            ws = sorted(pts.keys())
            for j in range(4):
                for wi, w_ in enumerate(ws):
                    pt, kb = pts[w_]
                    k.op(k.pe, lambda: nc.tensor.matmul(bov[:, j, 0:65], pt[:, j, :], Vx[:, kb, ph, 0:65],
                                                        start=(wi == 0), stop=(wi == len(ws) - 1)),
                         [pt, Vx], [bo], inc=(j == 3 and wi == len(ws) - 1))
            dn = den[(n * 2 + ph) % 2]
            k.op(k.dve, lambda: nc.vector.tensor_scalar(dn[:], bov[:, :, 64], 1.0, None, ALU.add), [bo], [dn])
            k.op(k.dve, lambda: nc.vector.reciprocal(dn[:], dn[:]), [dn], [dn])
            k.op(k.dve, lambda: nc.vector.tensor_tensor(
                out=self.A_tok[:, n, ph * 256:(ph + 1) * 256].rearrange("p (h d) -> p h d", d=64),
                in0=bov[:, :, 0:64], in1=dn[:, :].unsqueeze(2).to_broadcast([128, 4, 64]), op=ALU.mult),
                [bo, dn], [self.A_tok])

        its = [(n, ph) for n in range(NB) for ph in range(2)]
        prev = None
        for it in its:
            pts = att_a(*it)
            if prev is not None:
                att_b(*prev)
            prev = (it[0], it[1], pts)
        att_b(*prev)
        if "A_tok" in self.debug:
            self.dump("A_tok", self.A_tok, [NB, 512])
        k.release(mk)
        k.release_high(self.h_mark_wl)

    def p3_consts(self):
        k, nc = self.k, self.nc
        self.mk_p3c = k.mark()
        WL = self.load_const("w_lora", [512], BF16)
        Wg = self.load_const("w_gate", [512], BF16)
        self.mixc = self.load_const("mixc", [14])
        rwc = self.load_const("rw_cols", [20])
        lnx = self.load_const("lnx_b", [2, 512])
        rmask = self.load_const("rmask", [384], BF16)
        ind2 = self.load_const("ind2", [2], BF16)
        blk = self.load_const("blkb", [128], BF16, src=self.din["blk"])
        self.omix = k.alloc("omix", [14], F32)
        k.op(k.dve, lambda: nc.vector.tensor_scalar(self.omix[:], self.mixc[:], -1.0, 1.0, ALU.mult, ALU.add), [self.mixc], [self.omix])
        tiny = k.alloc("tiny", [1], F32)
        k.op(k.pool, lambda: nc.gpsimd.memset(tiny[:], 1e-30), [], [tiny])
        omka = k.alloc("omka", [4], F32)
        k.op(k.dve, lambda: nc.vector.tensor_scalar(omka[:], rwc[:, 12:16], -1.0, 1.0, ALU.mult, ALU.add), [rwc], [omka])
        self.C3 = dict(WL=WL, Wg=Wg, rwc=rwc, lnx=lnx, rmask=rmask, ind2=ind2, blk=blk, omka=omka, tiny=tiny)

    def shift_chunk(self, w, wcols, mixcol, F0, F1, dst_fn):
        k, nc = self.k, self.nc
        k.op(k.pool, lambda: nc.gpsimd.memset(F0[:, 0:1], 0.0), [], [F0])
        self.proj_fm(w, wcols, lambda tq, bk: copy_on(k, k.ev(), F0[:, 1 + tq * 512:1 + (tq + 1) * 512], bk[:, :], [bk], [F0]))
        k.op(k.act, lambda: nc.scalar.activation(out=F1[:, 0:S], in_=F0[:, 1:S + 1], func=AF.Copy,
                                                 scale=self.omix[:, mixcol:mixcol + 1]), [F0, self.omix], [F1])
        dst_fn(F0[:, 0:S], self.mixc[:, mixcol:mixcol + 1])

    def phase3(self):
        k, nc = self.k, self.nc
        CW = 0.6065306597126334
        mk = k.mark()
        sem_w = k.dma_sem("wl")
        w_src = self.din["w_in"].rearrange("(kc p) n -> p kc n", p=128)
        C3 = self.C3
        WL, Wg, rwc, lnx, rmask, ind2, blk, omka, tiny = (C3[n_] for n_ in ("WL", "Wg", "rwc", "lnx", "rmask", "ind2", "blk", "omka", "tiny"))
        L0T = k.alloc("L0T", [S], BF16)
        L1T = k.alloc("L1T", [S], BF16)
        mk_hp = k.mark()
        wl = self.wl

        F = [k.alloc(f"F{i}", [S + 1], F32) for i in range(2)]
        F2 = k.alloc("F2s", [S], F32)
        self.shift_chunk(wl, slice(0, 128), 0, F[0], F[1],
                         lambda prev, mix: k.op(k.dve, lambda: nc.vector.scalar_tensor_tensor(
                             out=F2[:], in0=prev, scalar=mix, in1=F[1][:, 0:S], op0=ALU.mult, op1=ALU.add),
                             [F[0], F[1], self.mixc], [F2]))
        k.op(k.act, lambda: nc.scalar.activation(out=L0T[0:64, :], in_=F2[0:64, :], func=AF.Tanh), [F2], [L0T])
        k.op(k.act, lambda: nc.scalar.activation(out=L0T[64:128, :], in_=F2[64:128, :], func=AF.Copy), [F2], [L0T])
        self.shift_chunk(wl, slice(128, 256), 1, F[0], F[1],
                         lambda prev, mix: k.op(k.dve, lambda: nc.vector.scalar_tensor_tensor(
                             out=F2[:], in0=prev, scalar=mix, in1=F[1][:, 0:S], op0=ALU.mult, op1=ALU.add),
                             [F[0], F[1], self.mixc], [F2]))
        k.op(k.act, lambda: nc.scalar.activation(out=L1T[:], in_=F2[:], func=AF.Sigmoid), [F2], [L1T])
        k.release(mk_hp)
        k.release_high(self.h_mark_hT)
        if self.upto == "p3a":
            if "L0T" in self.debug:
                self.dump("L0T", L0T, [S]); self.dump("L1T", L1T, [S])
            return

        ps = self.ps
        msk_su_iu = rmask[:, 0:256]
        msk_sl = rmask[:, 256:384]
        v3 = lambda ap: ap.rearrange("p (c n) -> p c n", n=128)

        NQ = int(os.environ.get("K_NQ", "4"))

        def prep_bufs():
            wr = k.alloc("wr", [8, 384], BF16)
            Fb = [k.alloc(f"F{i}", [S + 1], F32) for i in range(6)]
            FH = [[Buf(f_.ap, f"{f_.name}h{h}") for h in range(NQ)] for f_ in Fb]
            for f_, fh_ in zip(Fb, FH):
                for c_ in fh_:
                    c_.readers = dict(f_.readers)
            return wr, Fb, FH

        def prep(hp, P, wr, Fb, FH):
            c4 = slice(hp * 128, (hp + 1) * 128)
            col = lambda q: rwc[:, 4 * q + hp:4 * q + hp + 1]
            ar, kT, bT, prod, vs, gC = P["ar"], P["kT"], P["bT"], P["prod"], P["vs"], P["gC"]
            for h4 in range(2):
                k.dma(k.pool, wr[:, h4 * 4:(h4 + 1) * 4, :], w_src[:, h4 * 4:(h4 + 1) * 4, 1024 + hp * 384:1024 + (hp + 1) * 384], writes=[wr], sem=sem_w)
            HS = S // NQ
            HC = NB // NQ
            TQ = 4 // NQ

            def proj_half(h, wcols, dst_fn, lhs_rows=None):
                for tq in range(h * TQ, (h + 1) * TQ):
                    bk = self.bank()
                    for kc in range(8):
                        k.op(k.pe, lambda: nc.tensor.matmul(bk[:, :], wr[:, kc, wcols], self.hT[:, kc, tq * 512:(tq + 1) * 512],
                                                            start=(kc == 0), stop=(kc == 7)), [wr, self.hT], [bk], inc=(kc == 7))
                    dst_fn(tq, bk)

            def shift_half(h, wcols, mixcol, iraw, it1, out_fn):
                Fr, Ft = Fb[iraw], Fb[it1]
                t0, t1 = h * HS, (h + 1) * HS
                if h == 0:
                    k.op(k.pool, lambda: nc.gpsimd.memset(Fr[:, 0:1], 0.0), [], [FH[iraw][0]])
                proj_half(h, wcols, lambda tq, bk: copy_on(k, k.act if (h % 2 == 0 or (h == 1 and os.environ.get('K_PQ', '1') == '1')) else k.dve, Fr[:, 1 + tq * 512:1 + (tq + 1) * 512], bk[:, :], [bk], [FH[iraw][h]]))
                k.op(k.act, lambda: nc.scalar.activation(out=Ft[:, t0:t1], in_=Fr[:, 1 + t0:1 + t1], func=AF.Copy,
                                                         scale=self.omix[:, mixcol:mixcol + 1]), [FH[iraw][h], self.omix], [FH[it1][h]])
                out_fn(Fr[:, t0:t1], self.mixc[:, mixcol:mixcol + 1], [FH[iraw][max(h - 1, 0)], FH[iraw][h], FH[it1][h], self.mixc])

            def v3h(ap):
                return ap.rearrange("p (c n) -> p c n", n=128)

            def steps(h):
                t0, t1 = h * HS, (h + 1) * HS
                T = slice(t0, t1)
                T1 = slice(t0 + 1, t1 + 1)
                C = slice(h * HC, (h + 1) * HC)
                F0, F1, F2, F3, F4, F5 = Fb
                B0, B1, B2, B3, B4, B5 = [FH[i][h] for i in range(6)]
                sqb_ap = F0.ap.bitcast(BF16)
                blkb = blk
                for tq in range(h * TQ, (h + 1) * TQ):
                    bk = self.bank()
                    k.op(k.pe, lambda: nc.tensor.matmul(bk[:, :], WL[64:128, c4], L0T[64:128, tq * 512:(tq + 1) * 512], start=True, stop=True),
                         [WL, L0T], [bk])
                    k.op(k.act, lambda: nc.scalar.activation(out=F3[:, tq * 512:(tq + 1) * 512], in_=bk[:, :], func=AF.Sigmoid, bias=col(1)),
                         [bk, rwc], [B3])
                yield
                for tq in range(h * TQ, (h + 1) * TQ):
                    bk = self.bank()
                    k.op(k.pe, lambda: nc.tensor.matmul(bk[:, :], WL[0:64, c4], L0T[0:64, tq * 512:(tq + 1) * 512], start=True, stop=True),
                         [WL, L0T], [bk])
                    k.op(k.act, lambda: nc.scalar.activation(out=F5[:, tq * 512:(tq + 1) * 512], in_=bk[:, :], func=AF.Sigmoid, bias=col(0)),
                         [bk, rwc], [B5])
                yield
                if h == 0:
                    k.op(k.pool, lambda: nc.gpsimd.memset(F4[:, 0:1], 0.0), [], [FH[4][0]])
                    k.op(k.dve, lambda: nc.vector.tensor_tensor_scan(out=F4[:, T1], data0=self.zero1[:, 0:1].to_broadcast([128, HS]),
                                                                     data1=F5[:, T], initial=0.0, op0=ALU.add, op1=ALU.add),
                         [B5, self.zero1, FH[4][0]], [B4])
                else:
                    k.op(k.dve, lambda: nc.vector.tensor_tensor_scan(out=F4[:, T1], data0=self.zero1[:, 0:1].to_broadcast([128, HS]),
                                                                     data1=F5[:, T], initial=F4[:, t0:t0 + 1], op0=ALU.add, op1=ALU.add),
                         [B5, self.zero1, FH[4][h - 1]], [B4])
                yield
                shift_half(h, slice(128, 256), 2 + 3 * hp + 1, 0, 1,
                           lambda prev, mix, rd: k.op(k.dve, lambda: nc.vector.scalar_tensor_tensor(
                               out=F2[:, T], in0=prev, scalar=mix, in1=F1[:, T], op0=ALU.mult, op1=ALU.add), rd, [B2]))
                yield
                k.op(k.act, lambda: nc.scalar.activation(out=F1[:, T], in_=F2[:, T], func=AF.Copy, scale=col(2)), [B2, rwc], [B1])
                k.op(k.act, lambda: nc.scalar.activation(out=sqb_ap[:, T], in_=F1[:, T], func=AF.Square), [B1], list(FH[0]))
                yield
                for tq in range(h * TQ, (h + 1) * TQ):
                    bk = self.bank()
                    k.op(k.pe, lambda: nc.tensor.matmul(bk[:, :], blkb[:], sqb_ap[:, tq * 512:(tq + 1) * 512], start=True, stop=True), [blkb] + list(FH[0]), [bk])
                    k.op(k.act, lambda: nc.scalar.activation(out=F5[:, tq * 512:(tq + 1) * 512], in_=bk[:, :], func=AF.Ln, bias=tiny[:, 0:1]),
                         [bk, tiny], [B5])
                yield
                k.op(k.act, lambda: nc.scalar.activation(out=F5[:, T], in_=F5[:, T], func=AF.Exp, scale=-0.5), [B5], [B5])
                yield
                k.op(k.dve, lambda: nc.vector.tensor_tensor(out=F1[:, T], in0=F1[:, T], in1=F5[:, T], op=ALU.mult), [B1, B5], [B1])
                k.op(k.act, lambda: nc.scalar.activation(out=F0[:, T], in_=F3[:, T], func=AF.Identity, scale=col(3), bias=omka[:, hp:hp + 1]),
                     [B3, rwc, omka], [B0])
                yield
                k.op(k.dve, lambda: nc.vector.tensor_tensor(out=F2[:, T], in0=F2[:, T], in1=F0[:, T], op=ALU.mult), [B2, B0], [B2])
                k.op(k.dve, lambda: nc.vector.tensor_tensor(out=F3[:, T], in0=F3[:, T], in1=F1[:, T], op=ALU.mult), [B3, B1], [B3])
                yield
                base = v3h(F4[:, T])[:, :, 0:1].to_broadcast([128, HC, 128])
                rb = [FH[4][max(h - 1, 0)], B4]
                k.op(k.dve, lambda: nc.vector.tensor_tensor(out=v3h(F0[:, T]), in0=v3h(F4[:, T1]), in1=base, op=ALU.subtract), rb, [B0])
                yield
                k.op(k.act, lambda: nc.scalar.activation(out=F5[:, T], in_=F0[:, T], func=AF.Exp, scale=CW), [B0], [B5])
                yield
                k.op(k.dve, lambda: nc.vector.tensor_tensor(out=kT[:, T], in0=F2[:, T], in1=F5[:, T], op=ALU.mult), [B2, B5], [kT])
                k.op(k.dve, lambda: nc.vector.tensor_tensor(out=bT[:, T], in0=F3[:, T], in1=F5[:, T], op=ALU.mult), [B3, B5], [bT])
                yield
                k.op(k.act, lambda: nc.scalar.activation(out=F5[:, T], in_=F0[:, T], func=AF.Exp, scale=-CW), [B0], [B5])
                k.op(k.dve, lambda: nc.vector.tensor_copy(gC[:, C], v3h(F5[:, T])[:, :, 127]), [B5], [gC])
                yield
                shift_half(h, slice(0, 128), 2 + 3 * hp + 0, 3, 0,
                           lambda prev, mix, rd: k.op(k.dve, lambda: nc.vector.scalar_tensor_tensor(
                               out=F0[:, T], in0=prev, scalar=mix, in1=F0[:, T], op0=ALU.mult, op1=ALU.add), rd, [B0]))
                yield
                k.op(k.dve, lambda: nc.vector.tensor_tensor(out=ar[:, 1, T], in0=F0[:, T], in1=F5[:, T], op=ALU.mult),
                     [B0, B5], [ar])
                k.op(k.dve, lambda: nc.vector.scalar_tensor_tensor(out=prod[:, T], in0=F0[:, T], scalar=col(4), in1=F2[:, T],
                                                                   op0=ALU.mult, op1=ALU.mult), [B0, B2, rwc], [prod])
                yield
                k.op(k.dve, lambda: nc.vector.tensor_tensor(out=v3h(F0[:, T]), in0=v3h(F4[:, T]), in1=base, op=ALU.subtract), rb, [B0])
                yield
                k.op(k.act, lambda: nc.scalar.activation(out=F5[:, T], in_=F0[:, T], func=AF.Exp, scale=-CW), [B0], [B5])
                yield
                k.op(k.dve, lambda: nc.vector.scalar_tensor_tensor(out=ar[:, 0, T], in0=F1[:, T], scalar=-1.0, in1=F5[:, T],
                                                                   op0=ALU.mult, op1=ALU.mult), [B1, B5], [ar])
                yield
                shift_half(h, slice(256, 384), 2 + 3 * hp + 2, 3, 0,
                           lambda prev, mix, rd: k.op(k.dve, lambda: nc.vector.scalar_tensor_tensor(
                               out=vs[:, T], in0=prev, scalar=mix, in1=F0[:, T], op0=ALU.mult, op1=ALU.add), rd, [vs]))

            gens = [steps(h_) for h_ in range(NQ)]
            while gens:
                for g_ in list(gens):
                    try:
                        next(g_)
                    except StopIteration:
                        gens.remove(g_)

        def transposes(P):
            kT, bT, vs, tokm = P["kT"], P["bT"], P["vs"], P["tokm"]
            for c in range(0, NB, 2):
                bk = self.bank()
                pb = bk.ap.bitcast(BF16)
                for cc in range(2):
                    cs_ = slice((c + cc) * 128, (c + cc + 1) * 128)
                    for i, srcb in enumerate((kT, bT, vs)):
                        o_ = (cc * 3 + i) * 128
                        k.op(k.pe, lambda: nc.tensor.transpose(out=pb[:, o_:o_ + 128], in_=srcb[:, cs_], identity=self.identb[:]),
                             [srcb, self.identb], [bk], inc=(cc == 1 and i == 2))
                copy_on(k, k.ev(), tokm[:, c:c + 2].rearrange("p c a b -> p (c a b)"), pb[:, 0:768], [bk], [tokm])

        FILL = int(os.environ.get("K_FILL", "0"))
        fstate = {"first": True}

        def fill(n):
            for _ in range(n):
                if fstate["first"]:
                    k.op(k.pe, lambda: nc.tensor.matmul(ps[1][:, 0:128], self.identb[:], self.identb[:], start=True, stop=True),
                         [self.identb], [ps[1]])
                    fstate["first"] = False
                else:
                    k.op(k.pe, lambda: nc.tensor.matmul(ps[1][:, 0:128], self.identb[:], self.identb[:], start=True, stop=True), [], [], inc=False)

        def pre(c, Ps):
            par = c % 2
            heads = [(P, h) for P in Ps for h in range(2)]
            cs_ = slice(c * 128, (c + 1) * 128)
            eng_of = lambda hi: (k.act, k.act) if hi % 2 == 0 else (k.dve, k.dve)
            for hi, (P, h) in enumerate(heads):
                rows = slice(h * 64, h * 64 + 64)
                ar, kT, bT = P["ar"], P["kT"], P["bT"]
                bD, bB = ps[2 + hi], ps[0]
                bo = (hi % 2) * 256
                for w_ in range(2):
                    k.op(k.pe, lambda: nc.tensor.matmul(bD[:, w_ * 128:(w_ + 1) * 128], bT[rows, cs_], ar[rows, w_, cs_], start=True, stop=True),
                         [bT, ar], [bD], inc=False)
                k.op(k.pe, lambda: nc.tensor.matmul(bD[:, 256:384], ar[rows, 0, cs_], bT[rows, cs_], start=True, stop=True), [bT, ar], [bD])
                for w_ in range(2):
                    k.op(k.pe, lambda: nc.tensor.matmul(bB[:, bo + w_ * 128:bo + (w_ + 1) * 128], kT[rows, cs_], ar[rows, w_, cs_], start=True, stop=True),
                         [kT, ar], [bB], inc=(w_ == 1))
                fill(FILL)
                m1, m2, X0 = P["M1"][h][par], P["M2"][h][par], P["Xb"][h][0]
                k.op(k.dve, lambda: nc.vector.tensor_tensor(out=m1[:].rearrange("p a b -> p (a b)"), in0=bD[:, 0:256], in1=msk_su_iu, op=ALU.mult),
                     [bD, rmask], [m1])
                k.op(k.dve, lambda: nc.vector.tensor_tensor(out=X0[:], in0=bD[:, 256:384], in1=msk_sl, op=ALU.mult), [bD, rmask], [X0])
                k.op(k.dve, lambda: nc.vector.tensor_tensor(out=m2[:].rearrange("p a b -> p (a b)"), in0=bB[:, bo:bo + 256], in1=msk_su_iu, op=ALU.mult),
                     [bB, rmask], [m2])
            yield
            for hi, (P, h) in enumerate(heads):
                bD = ps[2 + hi]
                m1, X0 = P["M1"][h][par], P["Xb"][h][0]
                yt1 = P["YT"][h][0]
                e_big, e_small = eng_of(hi)
                k.op(k.pe, lambda: nc.tensor.matmul(bD[:, 0:128], X0[:], m1[:, 0, :], start=True, stop=True), [X0, m1], [bD], inc=False)
                k.op(k.pe, lambda: nc.tensor.matmul(bD[:, 256:384], m1[:, 0, :], X0[:], start=True, stop=True), [X0, m1], [bD])
                fill(FILL)
                copy_on(k, e_big, yt1[:, 0, :], bD[:, 0:128], [bD], [yt1])
                k.op(k.dve, lambda: nc.vector.tensor_tensor(out=yt1[:, 1, :], in0=m1[:, 0, :], in1=self.identb[:], op=ALU.add), [m1, self.identb], [yt1])
                copy_on(k, e_small, P["Xb"][h][1][:], bD[:, 256:384], [bD], [P["Xb"][h][1]])
            xi, yi = 1, 0
            p_ = 2
            while p_ < 128:
                yield
                last = (p_ * 2 >= 128)
                for hi, (P, h) in enumerate(heads):
                    bD = ps[2 + hi]
                    Xp, ytp = P["Xb"][h][xi], P["YT"][h][yi]
                    ytn = P["YT"][h][(yi + 1) % 3]
                    e_big, e_small = eng_of(hi)
                    if not last:
                        need_y = (p_ * 4 < 128)
                        if need_y:
                            k.op(k.pe, lambda: nc.tensor.matmul(bD[:, 0:128], Xp[:], ytp[:, 0, :], start=True, stop=True), [Xp, ytp], [bD], inc=False)
                        k.op(k.pe, lambda: nc.tensor.matmul(bD[:, 128:256], Xp[:], ytp[:, 1, :], start=True, stop=False), [Xp, ytp], [bD], inc=False)
                        k.op(k.pe, lambda: nc.tensor.matmul(bD[:, 128:256], self.identb[:], ytp[:, 1, :], start=False, stop=True),
                             [self.identb, ytp], [bD], inc=False)
                        k.op(k.pe, lambda: nc.tensor.matmul(bD[:, 256:384], ytp[:, 0, :], Xp[:], start=True, stop=True), [Xp, ytp], [bD])
                        fill(FILL)
                        if need_y:
                            copy_on(k, e_big, ytn[:].rearrange("p a b -> p (a b)"), bD[:, 0:256], [bD], [ytn])
                        else:
                            copy_on(k, e_big, ytn[:, 1, :], bD[:, 128:256], [bD], [ytn])
                        copy_on(k, e_small, P["Xb"][h][1 - xi][:], bD[:, 256:384], [bD], [P["Xb"][h][1 - xi]])
                    else:
                        tf = P["TTf"][h][par]
                        k.op(k.pe, lambda: nc.tensor.matmul(bD[:, 128:256], Xp[:], ytp[:, 1, :], start=True, stop=False), [Xp, ytp], [bD], inc=False)
                        k.op(k.pe, lambda: nc.tensor.matmul(bD[:, 128:256], self.identb[:], ytp[:, 1, :], start=False, stop=True),
                             [self.identb, ytp], [bD])
                        fill(FILL)
                        copy_on(k, e_big, tf[:], bD[:, 128:256], [bD], [tf])
                xi = 1 - xi
                yi = (yi + 1) % 3
                p_ *= 2

        def seq(c, Ps):
            par = c % 2
            hr = lambda h: slice(h * 64, h * 64 + 64)
            for q, P in enumerate(Ps):
                bS = ps[6 + q]
                for h in range(2):
                    k.op(k.pe, lambda: nc.tensor.matmul(bS[:, hr(h)], P["M2"][h][par][:, 0, :], P["tokm"][:, c, 2, hr(h)], start=True, stop=False),
                         [P["M2"][h][par], P["tokm"]], [bS], inc=False)
                    k.op(k.pe, lambda: nc.tensor.matmul(bS[:, hr(h)], P["ar"][hr(h), 0, c * 128:(c + 1) * 128], P["Hb"][hr(h), :], start=False, stop=True),
                         [P["ar"], P["Hb"]], [bS], inc=(h == 1))
            for q, P in enumerate(Ps):
                bS = ps[6 + q]
                copy_on(k, k.act if q == 0 else k.dve, P["Zs"][:], bS[:, 0:128], [bS], [P["Zs"]])
            yield
            for q, P in enumerate(Ps):
                bS = ps[6 + q]
                for h in range(2):
                    k.op(k.pe, lambda: nc.tensor.matmul(bS[:, 128 + h * 64:192 + h * 64], P["TTf"][h][par][:], P["Zs"][:, hr(h)], start=True, stop=True),
                         [P["TTf"][h][par], P["Zs"]], [bS], inc=(h == 1))
            for q, P in enumerate(Ps):
                bS = ps[6 + q]
                copy_on(k, k.act if q == 0 else k.dve, P["Us"][:], bS[:, 128:256], [bS], [P["Us"]])
            yield
            for q, P in enumerate(Ps):
                bS = ps[6 + q]
                bO = ps[1]
                for h in range(2):
                    oc = slice(q * 128 + h * 64, q * 128 + h * 64 + 64)
                    k.op(k.pe, lambda: nc.tensor.matmul(bO[:, oc], P["M2"][h][par][:, 1, :], P["tokm"][:, c, 2, hr(h)], start=True, stop=False),
                         [P["M2"][h][par], P["tokm"]], [bO], inc=False)
                    k.op(k.pe, lambda: nc.tensor.matmul(bO[:, oc], P["ar"][hr(h), 1, c * 128:(c + 1) * 128], P["Hb"][hr(h), :], start=False, stop=False),
                         [P["ar"], P["Hb"]], [bO], inc=False)
                    k.op(k.pe, lambda: nc.tensor.matmul(bO[:, oc], P["M1"][h][par][:, 1, :], P["Us"][:, hr(h)], start=False, stop=True),
                         [P["M1"][h][par], P["Us"]], [bO], inc=False)
                for h in range(2):
                    k.op(k.pe, lambda: nc.tensor.matmul(bS[hr(h), 384:448], P["tokm"][:, c, 0, hr(h)], P["tokm"][:, c, 2, hr(h)], start=True, stop=False),
                         [P["tokm"]], [bS], inc=False)
                    k.op(k.pe, lambda: nc.tensor.matmul(bS[hr(h), 384:448], P["tokm"][:, c, 1, hr(h)], P["Us"][:, hr(h)], start=False, stop=True),
                         [P["tokm"], P["Us"]], [bS], inc=(h == 1))
            yield
            for q, P in enumerate(Ps):
                bS = ps[6 + q]
                gCc = P["gC"][:, c:c + 1]
                k.op(k.dve, lambda: nc.vector.scalar_tensor_tensor(out=P["Hb"][:], in0=bS[:, 384:448], scalar=gCc, in1=P["Hg"][:],
                                                                   op0=ALU.mult, op1=ALU.add), [bS, P["gC"], P["Hg"]], [P["Hb"]])
                k.op(k.dve, lambda: nc.vector.scalar_tensor_tensor(out=P["Hf"][:], in0=bS[:, 384:448], scalar=gCc, in1=P["Hg"][:],
                                                                   op0=ALU.mult, op1=ALU.add), [bS, P["gC"], P["Hg"]], [P["Hf"]])
                if c + 1 < NB:
                    k.op(k.pool, lambda: nc.gpsimd.tensor_scalar(P["Hg"][:], P["Hf"][:], P["gC"][:, c + 1:c + 2], 0.0, ALU.mult, ALU.add),
                         [P["Hf"], P["gC"]], [P["Hg"]])
                k.op(k.act, lambda: nc.scalar.activation(out=P["o_tok"][:, c, :], in_=ps[1][:, q * 128:(q + 1) * 128], func=AF.Copy), [ps[1]], [P["o_tok"]])

        def pass4(hp, P):
            c4 = slice(hp * 128, (hp + 1) * 128)
            o_tok, tokm, prod = P["o_tok"], P["tokm"], P["prod"]
            mk4 = k.mark()
            G0 = k.alloc("G0", [NB, 128], F32)
            G1 = k.alloc("G1", [NB, 128], F32)
            st = k.alloc("st", [4, 32], F32)
            o3 = lambda b: b[:].rearrange("p c (h d) -> p (c h) d", d=64)
            bc = lambda ap: ap.unsqueeze(2).to_broadcast([128, 32, 64])
            k.op(k.dve, lambda: nc.vector.tensor_reduce(out=st[:, 0, :], in_=o3(o_tok), axis=AX.X, op=ALU.add), [o_tok], [st])
            k.op(k.act, lambda: nc.scalar.activation(out=G0[:], in_=o_tok[:], func=AF.Square), [o_tok], [G0])
            k.op(k.dve, lambda: nc.vector.tensor_reduce(out=st[:, 1, :], in_=o3(G0), axis=AX.X, op=ALU.add), [G0], [st])
            k.op(k.dve, lambda: nc.vector.tensor_scalar(st[:, 0, :], st[:, 0, :], -1.0 / 64, None, ALU.mult), [st], [st])
            k.op(k.dve, lambda: nc.vector.tensor_tensor(out=st[:, 2, :], in0=st[:, 0, :], in1=st[:, 0, :], op=ALU.mult), [st], [st])
            k.op(k.dve, lambda: nc.vector.scalar_tensor_tensor(out=st[:, 1, :], in0=st[:, 1, :], scalar=1.0 / 64, in1=st[:, 2, :],
                                                               op0=ALU.mult, op1=ALU.subtract), [st], [st])
            k.op(k.dve, lambda: nc.vector.tensor_scalar(st[:, 1, :], st[:, 1, :], 64e-5, None, ALU.add), [st], [st])
            k.op(k.act, lambda: nc.scalar.activation(out=st[:, 1, :], in_=st[:, 1, :], func=AF.Sqrt), [st], [st])
            k.op(k.dve, lambda: nc.vector.reciprocal(st[:, 1, :], st[:, 1, :]), [st], [st])
            k.op(k.dve, lambda: nc.vector.tensor_tensor(out=o3(G0), in0=o3(o_tok), in1=bc(st[:, 0, :]), op=ALU.add), [o_tok, st], [G0])
            k.op(k.dve, lambda: nc.vector.tensor_tensor(out=o3(G0), in0=o3(G0), in1=bc(st[:, 1, :]), op=ALU.mult), [G0, st], [G0])
            lg = lnx[:, 0, c4].unsqueeze(1).to_broadcast([128, NB, 128])
            lb = lnx[:, 1, c4].unsqueeze(1).to_broadcast([128, NB, 128])
            k.op(k.dve, lambda: nc.vector.tensor_tensor(out=G0[:], in0=G0[:], in1=lg, op=ALU.mult), [G0, lnx], [G0])
            k.op(k.dve, lambda: nc.vector.tensor_tensor(out=G0[:], in0=G0[:], in1=lb, op=ALU.add), [G0, lnx], [G0])
            bk = self.bank()
            for c in range(NB):
                k.op(k.pe, lambda: nc.tensor.matmul(bk[:, 2 * c:2 * c + 2], prod[:, c * 128:(c + 1) * 128], ind2[:], start=True, stop=True),
                     [prod, ind2], [bk], inc=(c == NB - 1))
            k.op(k.act, lambda: nc.scalar.activation(out=st[:, 3, :], in_=bk[:, 0:32], func=AF.Copy), [bk], [st])
            k.op(k.dve, lambda: nc.vector.tensor_tensor(
                out=G1[:].rearrange("p c (h d) -> p c h d", d=64), in0=tokm[:, :, 2, :].rearrange("p c (h d) -> p c h d", d=64),
                in1=st[:, 3, :].rearrange("p (c h) -> p c h", h=2).unsqueeze(3).to_broadcast([128, NB, 2, 64]), op=ALU.mult),
                [tokm, st], [G1])
            k.op(k.dve, lambda: nc.vector.tensor_tensor(out=G0[:], in0=G0[:], in1=G1[:], op=ALU.add), [G0, G1], [G0])
            for g4 in range(4):
                bk = self.bank()
                for i in range(4):
                    c = g4 * 4 + i
                    k.op(k.pe, lambda: nc.tensor.matmul(bk[:, i * 128:(i + 1) * 128], L1T[:, c * 128:(c + 1) * 128], Wg[:, c4], start=True, stop=True),
                         [L1T, Wg], [bk], inc=(i == 3))
                k.op(k.dve, lambda: nc.vector.tensor_tensor(out=self.R_tok[:, g4 * 4:(g4 + 1) * 4, c4],
                                                            in0=G0[:, g4 * 4:(g4 + 1) * 4, :],
                                                            in1=bk[:, :].rearrange("p (c n) -> p c n", n=128), op=ALU.mult),
                     [G0, bk], [self.R_tok])
            k.release(mk4)

        NPAIR = 2
        for grp in range(4 // NPAIR):
            Ps = []
            for q in range(NPAIR):
                P = {}
                P["ar"] = k.alloc(f"ar{q}", [2, S], BF16)
                for nm in ("kT", "bT", "prod", "vs"):
                    P[nm] = k.alloc(f"{nm}{q}", [S], BF16)
                P["gC"] = k.alloc(f"gC{q}", [NB], F32)
                P["Hf"] = k.alloc(f"Hf{q}", [64], F32)
                P["Hg"] = k.alloc(f"Hg{q}", [64], F32)
                P["Hb"] = k.alloc(f"Hb{q}", [64], BF16)
                P["M1"] = [[k.alloc(f"M1_{q}{h}{p}", [2, 128], BF16) for p in range(2)] for h in range(2)]
                P["M2"] = [[k.alloc(f"M2_{q}{h}{p}", [2, 128], BF16) for p in range(2)] for h in range(2)]
                P["Xb"] = [[k.alloc(f"X_{q}{h}{p}", [128], BF16) for p in range(2)] for h in range(2)]
                P["YT"] = [[k.alloc(f"YT_{q}{h}{p}", [2, 128], BF16) for p in range(3)] for h in range(2)]
                P["TTf"] = [[k.alloc(f"TTf_{q}{h}{p}", [128], BF16) for p in range(2)] for h in range(2)]
                P["Zs"] = k.alloc(f"Zs{q}", [128], BF16)
                P["Us"] = k.alloc(f"Us{q}", [128], BF16)
                Ps.append(P)
            mk_f = k.mark()
            wr_, Fb_, FH_ = prep_bufs()
            for q, P in enumerate(Ps):
                prep(grp * NPAIR + q, P, wr_, Fb_, FH_)
            for f_, fh_ in zip(Fb_, FH_):
                k.adopt(f_, fh_)
            k.release(mk_f)
            if grp == 4 // NPAIR - 1:
                self.Wo = Buf(self.hT.ap.rearrange("p a b -> p (a b)")[:, 0:8 * D].rearrange("p (a b) -> p a b", b=D), "Wo")
                sem_wo = k.dma_sem("wout")
                wo_src = self.din["w_out"].rearrange("(kc p) n -> p kc n", p=128)
                for h4 in range(2):
                    k.dma(k.pool, self.Wo[:, h4 * 4:(h4 + 1) * 4, :], wo_src[:, h4 * 4:(h4 + 1) * 4, :], writes=[self.Wo, self.hT], sem=sem_wo)
            for q, P in enumerate(Ps):
                P["tokm"] = k.alloc(f"tokm{q}", [NB, 3, 128], BF16)
                P["o_tok"] = k.alloc(f"o_tok{q}", [NB, 128], F32)
                transposes(P)
                k.op(k.pool, lambda: nc.gpsimd.memset(P["Hf"][:], 0.0), [], [P["Hf"]])
                k.op(k.pool, lambda: nc.gpsimd.memset(P["Hb"][:], 0.0), [], [P["Hb"]])
                k.op(k.pool, lambda: nc.gpsimd.memset(P["Hg"][:], 0.0), [], [P["Hg"]])
            fstate["first"] = True
            for _ in pre(0, Ps):
                pass
            for c in range(NB):
                gens = [seq(c, Ps)] + ([pre(c + 1, Ps)] if c + 1 < NB else [])
                sched = int(os.environ.get("K_SCHED", "1"))
                if sched == 0:
                    for g_ in reversed(gens):
                        for _ in g_:
                            pass
                else:
                    while gens:
                        for g_ in list(gens):
                            try:
                                next(g_)
                            except StopIteration:
                                gens.remove(g_)
            k.op(k.pe, lambda: nc.tensor.matmul(ps[1][:, 0:128], self.identb[:], self.identb[:], start=True, stop=True), [self.identb], [ps[1]])
            for q, P in enumerate(Ps):
                pass4(grp * NPAIR + q, P)
            k.release(mk_hp)

        if "R_tok" in self.debug:
            self.dump("R_tok", self.R_tok, [NB, 512])
        k.release(self.mk_p3c)

    def rstd_from_ss(self, ss2, rstd):
        k, nc = self.k, self.nc
        k.op(k.dve, lambda: nc.vector.tensor_tensor(out=rstd[:], in0=ss2[:, 0:1], in1=ss2[:, 1:2], op=ALU.add), [ss2], [rstd])
        k.op(k.dve, lambda: nc.vector.tensor_scalar(rstd[:], rstd[:], 1.0 / D, 1e-6, ALU.mult, ALU.add), [rstd], [rstd])
        k.op(k.act, lambda: nc.scalar.activation(out=rstd[:], in_=rstd[:], func=AF.Sqrt), [rstd], [rstd])
        k.op(k.dve, lambda: nc.vector.reciprocal(rstd[:], rstd[:]), [rstd], [rstd])

    def phase5(self):
        k, nc = self.k, self.nc
        self.x1 = [k.alloc(f"x1_{t}", [D], F32) for t in range(NB)]
        self.h2T = k.alloc("h2T", [8, 1024], BF16)
        self.n6 = dict(xn=[k.alloc(f"xn6_{i}", [D], F32) for i in range(2)], sq=k.alloc("sq6", [D], BF16),
                       ss=k.alloc("ss6", [8], F32), rs=k.alloc("rs6", [8], F32))
        self.ffn_prefetch()
        mk = k.mark()
        Wo = self.Wo
        gpost = self.load_const("g_post_b", [D])
        catT = [k.alloc(f"catT{i}", [8, 128], BF16) for i in range(3)]
        tmp = [k.alloc(f"tmp{i}", [D], F32) for i in range(2)]
        sq = k.alloc("sq5", [512], F32)
        ss2 = [k.alloc(f"ss2_{i}", [2], F32) for i in range(2)]
        rstd = [k.alloc(f"rstd5_{i}", [1], F32) for i in range(2)]
        xr = [k.alloc(f"xr{i}", [D], F32) for i in range(3)]

        def stage_a(n):
            xt = xr[n % 3]
            k.dma(k.sp, xt[:], self.x[n * 128:(n + 1) * 128, :], writes=[xt], sem=self.x_sems[n % 3])
            ct = catT[n % 3]
            bk = self.bank()
            pb = bk.ap.bitcast(BF16)
            for kc in range(8):
                src = self.A_tok if kc < 4 else self.R_tok
                cc = (kc % 4) * 128
                k.op(k.pe, lambda: nc.tensor.transpose(out=pb[:, kc * 128:(kc + 1) * 128], in_=src[:, n, cc:cc + 128], identity=self.identb[:]),
                     [src, self.identb], [bk], inc=(kc == 7))
            copy_on(k, k.act if os.environ.get("K_P5A", "1") == "1" else k.ev(), ct[:].rearrange("p a b -> p (a b)"), pb[:, 0:1024], [bk], [ct])

        def stage_b(n):
            xt = xr[n % 3]
            ct = catT[n % 3]
            banks = []
            for dh in range(2):
                bm = self.bank()
                for kc in range(8):
                    k.op(k.pe, lambda: nc.tensor.matmul(bm[:, :], ct[:, kc, :], Wo[:, kc, dh * 512:(dh + 1) * 512], start=(kc == 0), stop=(kc == 7)),
                         [ct, Wo], [bm], inc=(kc == 7))
                k.op(k.act, lambda: nc.scalar.activation(out=sq[:], in_=bm[:, :], func=AF.Square, accum_out=ss2[n % 2][:, dh:dh + 1]),
                     [bm], [sq, ss2[n % 2]])
                banks.append(bm)
            self.rstd_from_ss(ss2[n % 2], rstd[n % 2])
            tm = tmp[n % 2]
            for dh in range(2):
                k.op(k.dve, lambda: nc.vector.scalar_tensor_tensor(out=tm[:, dh * 512:(dh + 1) * 512], in0=banks[dh][:, :], scalar=rstd[n % 2][:, 0:1],
                                                                   in1=gpost[:, dh * 512:(dh + 1) * 512], op0=ALU.mult, op1=ALU.mult),
                     [banks[dh], rstd[n % 2], gpost], [tm])
            k.op(k.dve, lambda: nc.vector.tensor_tensor(out=self.x1[n][:], in0=tm[:], in1=xt[:], op=ALU.add), [tm, xt], [self.x1[n]])

        stage_a(0)
        hgen = None
        for n in range(NB):
            if n + 1 < NB:
                stage_a(n + 1)
            stage_b(n)
            if n == 8:
                hgen = self.h2_norm(0)
            if hgen is not None:
                next(hgen, None)
        for _ in hgen:
            pass
        if "x1" in self.debug:
            o = self.dbg_out("x1", [S, D])
            for n in range(NB):
                k.dma(k.sp, o[n * 128:(n + 1) * 128, :], self.x1[n][:], reads=[self.x1[n]], sem=self.sem_dbg)
        k.release(mk)

    def h2_norm(self, hf):
        n6 = self.n6
        akc = tuple(range(8)) if os.environ.get("K_H2A", "1") == "1" else None
        return self.norm_T_gen(lambda t, xt: None, self.g_ffn, self.h2T, 4, self.x1[hf * 8:(hf + 1) * 8], n6["xn"], n6["sq"], n6["ss"], n6["rs"],
                               act_kc=akc)

    def ffn_prefetch(self):
        k, nc = self.k, self.nc
        F_ = self.F6 = {}
        F_["cw"] = self.load_const("conv_w", [64, 3])
        F_["cb"] = self.load_const("conv_b", [64])
        F_["gfp"] = self.load_const("g_fpost_b", [D])
        F_["halo"] = k.alloc("halo", [64, 2], F32)
        k.op(k.pool, lambda: nc.gpsimd.memset(F_["halo"][:], 0.0), [], [F_["halo"]])
        F_["hl_t"] = [k.alloc(f"hl_t{i}", [2], F32) for i in range(2)]
        NUP = 3
        F_["sem_up"] = [k.dma_sem(f"wup{i}") for i in range(NUP)]
        F_["sem_dn"] = [k.dma_sem(f"wdn{i}") for i in range(2)]
        F_["wup"] = [k.alloc(f"wup{i}", [8, 256], BF16) for i in range(NUP)]
        self.load_up(0)
        self.load_up(1)

    def load_up(self, ii):
        k = self.k
        F_ = self.F6
        up_src = self.din["w_up"].rearrange("(kc p) n -> p kc n", p=128)
        i = ii % 32
        wu = F_["wup"][ii % 3]
        for h4 in range(2):
            k.dma(k.pool, wu[:, h4 * 4:(h4 + 1) * 4, :], up_src[:, h4 * 4:(h4 + 1) * 4, i * 256:(i + 1) * 256], writes=[wu], sem=F_["sem_up"][ii % 3])

    def load_dn(self, gg):
        k = self.k
        F_ = self.F6
        dn_src = self.din["w_down"].rearrange("(c p) n -> p c n", p=128)
        g = gg % 4
        w = F_["wd"][gg % 2]
        for h4 in range(2):
            k.dma(k.pool, w[:, h4 * 4:(h4 + 1) * 4, :], dn_src[:, g * 8 + h4 * 4:g * 8 + (h4 + 1) * 4, :], writes=[w], sem=F_["sem_dn"][gg % 2])

    def phase6(self):
        k, nc = self.k, self.nc
        k.adopt(self.hT, [self.Wo])
        k.release_high(k.cap)
        mk = k.mark()
        F_ = self.F6
        F_["wd"] = [k.alloc(f"wd{i}", [8, D], BF16) for i in range(2)]
        self.load_dn(0)
        cw, cb, gfp, halo, hl_t, wup, wd = F_["cw"], F_["cb"], F_["gfp"], F_["halo"], F_["hl_t"], F_["wup"], F_["wd"]
        NUP = 3
        load_up, load_dn = self.load_up, self.load_dn
        NGT = 9
        h2T = self.h2T
        GT = [k.alloc(f"GT{c}", [1024], BF16) for c in range(NGT)]
        f = [k.alloc(f"f{t}", [D], F32) for t in range(8)]
        Cb = [k.alloc(f"Cb{i}", [512], F32) for i in range(5)]
        Gg = [k.alloc(f"Gg{i}", [512], F32) for i in range(2)]
        ssf = [k.alloc(f"ssf{i}", [1], F32) for i in range(2)]
        st = {"ui": 0}

        def down(hf, g):
            w = wd[(hf * 4 + g) % 2]
            for tt in range(8):
                for dh in range(2):
                    bk = self.bank()
                    for ci in range(8):
                        gt = GT[(hf * 32 + g * 8 + ci) % NGT]
                        k.op(k.pe, lambda: nc.tensor.matmul(bk[:, :], gt[:, tt * 128:(tt + 1) * 128], w[:, ci, dh * 512:(dh + 1) * 512],
                                                            start=(ci == 0), stop=(ci == 7)), [gt, w], [bk], inc=(ci == 7))
                    fs = f[tt][:, dh * 512:(dh + 1) * 512]
                    if g == 0:
                        k.op(k.act, lambda: nc.scalar.activation(out=fs, in_=bk[:, :], func=AF.Copy), [bk], [f[tt]])
                    else:
                        k.op(k.dve, lambda: nc.vector.tensor_tensor(out=fs, in0=bk[:, :], in1=fs, op=ALU.add), [bk, f[tt]], [f[tt]])
                yield tt

        def final(hf, tts=range(8)):
            for tt in tts:
                n = hf * 8 + tt
                s1 = ssf[tt % 2]
                sqj = self.n6["sq"]
                k.op(k.act, lambda: nc.scalar.activation(out=sqj[:], in_=f[tt][:], func=AF.Square, accum_out=s1[:, 0:1]), [f[tt]], [sqj, s1])
                k.op(k.dve, lambda: nc.vector.tensor_scalar(s1[:], s1[:], 1.0 / D, 1e-6, ALU.mult, ALU.add), [s1], [s1])
                k.op(k.act, lambda: nc.scalar.activation(out=s1[:], in_=s1[:], func=AF.Sqrt), [s1], [s1])
                k.op(k.dve, lambda: nc.vector.reciprocal(s1[:], s1[:]), [s1], [s1])
                o_ = self.n6["xn"][tt % 2]
                k.op(k.dve, lambda: nc.vector.scalar_tensor_tensor(out=o_[:], in0=f[tt][:], scalar=s1[:, 0:1], in1=gfp[:], op0=ALU.mult, op1=ALU.mult),
                     [f[tt], s1, gfp], [o_])
                k.op(k.dve, lambda: nc.vector.tensor_tensor(out=o_[:], in0=o_[:], in1=self.x1[n][:], op=ALU.add), [o_, self.x1[n]], [o_])
                k.dma(k.sp, self.out[n * 128:(n + 1) * 128, :], o_[:], reads=[o_], sem=self.sem_o2[tt % 2])

        def up(hf, i):
            g, ci = divmod(i, 8)
            ii = hf * 32 + i
            if ci == 0 and ii > 0:
                load_dn(hf * 4 + g)
            if ii + 2 < 64:
                load_up(ii + 2)
            wu = wup[ii % NUP]
            gtb = GT[ii % NGT]
            prev_bk = {}
            for tq in range(2):
                res = {}
                for gv in range(2):
                    c2 = 2 * i + gv
                    bk = self.bank()
                    for kc in range(8):
                        k.op(k.pe, lambda: nc.tensor.matmul(bk[:, :], wu[:, kc, gv * 128:(gv + 1) * 128], h2T[:, kc, tq * 512:(tq + 1) * 512],
                                                            start=(kc == 0), stop=(kc == 7)), [wu, h2T], [bk], inc=(kc == 7))
                    c = Cb[st["ui"] % 5]
                    st["ui"] += 1
                    w0, w1, w2 = cw[:, c2, 0:1], cw[:, c2, 1:2], cw[:, c2, 2:3]
                    k.op(k.act, lambda: nc.scalar.activation(out=c[:], in_=bk[:, :], func=AF.Identity, scale=w2, bias=cb[:, c2:c2 + 1]),
                         [bk, cw, cb], [c])
                    k.op(k.dve, lambda: nc.vector.scalar_tensor_tensor(out=c[:, 1:512], in0=bk[:, 0:511], scalar=w1, in1=c[:, 1:512],
                                                                       op0=ALU.mult, op1=ALU.add), [bk, cw, c], [c])
                    k.op(k.dve, lambda: nc.vector.scalar_tensor_tensor(out=c[:, 2:512], in0=bk[:, 0:510], scalar=w0, in1=c[:, 2:512],
                                                                       op0=ALU.mult, op1=ALU.add), [bk, cw, c], [c])
                    if tq == 0:
                        hsrc, hb = (halo[:, c2, :], halo) if hf == 1 else (None, None)
                        prev_bk[gv] = bk
                    else:
                        hsrc, hb = prev_bk[gv][:, 510:512], prev_bk[gv]
                    if hsrc is not None:
                        k.op(k.dve, lambda: nc.vector.scalar_tensor_tensor(out=c[:, 0:2], in0=hsrc, scalar=w0, in1=c[:, 0:2],
                                                                           op0=ALU.mult, op1=ALU.add), [hb, cw, c], [c])
                        k.op(k.dve, lambda: nc.vector.scalar_tensor_tensor(out=c[:, 0:1], in0=hsrc[:, 1:2], scalar=w1, in1=c[:, 0:1],
                                                                           op0=ALU.mult, op1=ALU.add), [hb, cw, c], [c])
                    if tq == 1 and hf == 0:
                        k.op(k.dve, lambda: nc.vector.tensor_copy(halo[:, c2, :], bk[:, 510:512]), [bk], [halo])
                    res[gv] = c
                gg = Gg[(i * 2 + tq) % 2]
                k.op(k.act, lambda: nc.scalar.activation(out=gg[:], in_=res[0][:], func=AF.Gelu_apprx_tanh), [res[0]], [gg])
                k.op(k.pool, lambda: nc.gpsimd.tensor_tensor(out=gtb[:, tq * 512:(tq + 1) * 512], in0=gg[:], in1=res[1][:], op=ALU.mult),
                     [gg, res[1]], [gtb])

        for hf in range(2):
            for i in range(32):
                up(hf, i)
                if i % 8 == 0 and i > 0:
                    for _ in down(hf, i // 8 - 1):
                        pass
            if hf == 0:
                hg = self.h2_norm(1)
                dg = down(0, 3)
                next(hg, None)
                alive = True
                while alive:
                    alive = False
                    for _ in range(2):
                        if next(dg, None) is not None:
                            alive = True
                    for _ in range(2):
                        if next(hg, None) is not None:
                            alive = True
                final(0)
        for tt_done in down(1, 3):
            final(1, [tt_done])
        k.release(mk)


_CACHE = {}


def kernel(**inputs):
    sh = host_layout(inputs)
    x = np.asarray(inputs["x"], dtype=np.float32)
    B = x.shape[0]
    if "prog" not in _CACHE:
        p = Prog()
        p.build()
        _CACHE["prog"] = p
    p = _CACHE["prog"]
    in_maps = [dict(sh, x=np.ascontiguousarray(x[b])) for b in range(B)]
    res = run_bass_kernel_spmd(p.nc, in_maps, core_ids=list(range(B)))
    return np.stack([np.asarray(r["out"], dtype=np.float32) for r in res.results], axis=0)
```

```python
import math
import os
import numpy as np
import concourse.bass as bass
import concourse.mybir as mybir
from concourse.bass_utils import run_bass_kernel_spmd

F32 = mybir.dt.float32
BF16 = mybir.dt.bfloat16
U8 = mybir.dt.uint8
AF = mybir.ActivationFunctionType
ALU = mybir.AluOpType
AX = mybir.AxisListType

S = 2048
D = 1024
NB = 16
DSZ = {F32: 4, BF16: 2, U8: 1}
SOFT_RELEASE = os.environ.get("K_SOFT", "1") == "1"
EMBED_WAIT = tuple(x for x in os.environ.get("K_EMBED", "pe,act,dve,pool").split(",") if x)


class Buf:
    __slots__ = ("ap", "last_w", "readers", "name", "excl")

    def __init__(self, ap, name="", excl=False):
        self.ap = ap
        self.excl = excl
        self.last_w = []
        self.readers = {}
        self.name = name

    def __getitem__(self, k):
        return self.ap[k]


class Eng:
    def __init__(self, name, h, sem):
        self.name = name
        self.h = h
        self.sem = sem
        self.count = 0
        self.seen = {}
        self.log = []
        self.pend = []


class K:
    def __init__(self, nc):
        self.nc = nc
        self.sems = []
        self.pe = self._eng("pe", nc.tensor)
        self.act = self._eng("act", nc.scalar)
        self.dve = self._eng("dve", nc.vector)
        self.pool = self._eng("pool", nc.gpsimd)
        self.sp = self._eng("sp", nc.sync)
        self.engs = [self.pe, self.act, self.dve, self.pool, self.sp]
        self.dma_sems = []
        self.rr = 0
        self.arena = None
        self.top = 0
        self.cap = 0
        self.ghosts = []
        self.live = []

    def _eng(self, name, h):
        sem = self.nc.alloc_semaphore(name="S_" + name)
        return Eng(name, h, sem)

    def set_arena(self, tensor, cap):
        self.arena = tensor
        self.cap = cap
        self.top = 0
        self.htop = cap

    def release_high(self, n_bytes_top, hard=False):
        if hard or not SOFT_RELEASE:
            self.barrier()
        self._ghost(self.htop, n_bytes_top)
        self.htop = n_bytes_top

    def alloc(self, name, shape, dtype, high=False):
        n = int(np.prod(shape)) * DSZ[dtype]
        n = (n + 63) // 64 * 64
        if high:
            self.htop -= n
            off = self.htop
            assert self.top <= self.htop, f"SBUF arena overflow allocating {name} (high)"
        else:
            off = self.top
            assert off + n <= self.htop, f"SBUF arena overflow allocating {name}: {off}+{n}>{self.htop}"
            self.top += n
        inherit = {}
        for (g0, g1, toks) in self.ghosts:
            if g0 < off + n and off < g1:
                for key, t in toks.items():
                    if key not in inherit or inherit[key][1] < t[1]:
                        inherit[key] = t
        self.ghosts = [g for g in self.ghosts if not (off <= g[0] and g[1] <= off + n)]
        self._pending_inherit = inherit
        self._pending_range = (off, off + n)
        ap = self.arena[:, off:off + n]
        if dtype != U8:
            ap = ap.bitcast(dtype)
        nel = int(np.prod(shape))
        ap = ap[:, 0:nel]
        if len(shape) == 2:
            ap = ap.rearrange("p (a b) -> p a b", b=shape[1])
        elif len(shape) == 3:
            ap = ap.rearrange("p (a b c) -> p a b c", b=shape[1], c=shape[2])
        elif len(shape) == 4:
            ap = ap.rearrange("p (a b c d) -> p a b c d", b=shape[1], c=shape[2], d=shape[3])
        b = Buf(ap, name)
        b.readers = dict(self._pending_inherit)
        self.live.append((self._pending_range[0], self._pending_range[1], b))
        return b

    def mark(self):
        return self.top

    @staticmethod
    def _toks_of(b):
        toks = {}
        for t in list(b.last_w) + list(b.readers.values()):
            if t[0].num not in toks or toks[t[0].num][1] < t[1]:
                toks[t[0].num] = t
        return toks

    def adopt(self, parent, children):
        for c in children:
            for key, t in self._toks_of(c).items():
                if key not in parent.readers or parent.readers[key][1] < t[1]:
                    parent.readers[key] = t

    def _ghost(self, lo, hi):
        keep = []
        for (a0, a1, b) in self.live:
            if a0 >= lo and a1 <= hi:
                self.ghosts.append((a0, a1, self._toks_of(b)))
            else:
                keep.append((a0, a1, b))
        self.live = keep

    def release(self, mark, hard=False):
        if hard or not SOFT_RELEASE:
            self.barrier()
        self._ghost(mark, self.top)
        self.top = mark

    def dma_sem(self, name):
        sem = self.nc.alloc_semaphore(name="D_" + name)
        ent = [sem, 0]
        self.dma_sems.append(ent)
        return ent

    def _wait(self, eng, toks):
        need = {}
        for (sem, val) in toks:
            key = sem.num
            for ent in self.dma_sems:
                if ent[0].num == key:
                    val = max(val, ent[1])
                    break
            if key not in need or need[key][1] < val:
                need[key] = (sem, val)
        for key, (sem, val) in need.items():
            if eng.seen.get(key, 0) < val:
                eng.h.wait_ge(sem, val)
                eng.seen[key] = val
                eng.pend.append((key, val))

    def _deps(self, eng, reads, writes):
        toks = []
        for b in reads:
            toks.extend(b.last_w)
        for b in writes:
            for t in b.last_w:
                if t[0].num != eng.sem.num:
                    toks.append(t)
            for key, t in b.readers.items():
                if key != eng.sem.num:
                    toks.append(t)
        return toks

    def op(self, eng, fn, reads=(), writes=(), inc=True):
        ex = [b for b in reads if b.excl]
        if ex:
            reads = [b for b in reads if not b.excl]
            writes = list(writes) + [b for b in ex if b not in writes]
        emb = None
        if EMBED_WAIT and eng.name in EMBED_WAIT:
            need = {}
            for (sem, val) in self._deps(eng, reads, writes):
                for ent in self.dma_sems:
                    if ent[0].num == sem.num:
                        val = max(val, ent[1])
                        break
                if eng.seen.get(sem.num, 0) < val and (sem.num not in need or need[sem.num][1] < val):
                    need[sem.num] = (sem, val)
            if need:
                keys = list(need.keys())
                emb = need[keys[-1]]
                rest = [need[k_] for k_ in keys[:-1]]
                self._wait(eng, rest)
                eng.seen[emb[0].num] = emb[1]
                eng.pend.append((emb[0].num, emb[1]))
        else:
            self._wait(eng, self._deps(eng, reads, writes))
        ins = fn()
        if emb is not None:
            ins._wait_ge(emb[0], emb[1])
        if inc:
            eng.count += 1
            ins.then_inc(eng.sem, 1)
            tok = (eng.sem, eng.count)
            eng.log.append((eng.pend, [(eng.sem.num, 1)]))
        else:
            tok = (eng.sem, eng.count + 1)
            eng.log.append((eng.pend, []))
        eng.pend = []
        for b in writes:
            b.last_w = [tok]
            b.readers = {}
        for b in reads:
            if b in writes:
                continue
            b.readers[eng.sem.num] = tok
        return ins

    def dma(self, eng, out, in_, reads=(), writes=(), sem=None):
        self._wait(eng, self._deps(eng, reads, writes))
        if sem is None:
            sem = self.dma_sems[0]
        ins = eng.h.dma_start(out=out, in_=in_)
        sem[1] += 16
        ins.then_inc(sem[0], 16)
        eng.log.append((eng.pend, [(sem[0].num, 16)]))
        eng.pend = []
        tok = (sem[0], sem[1])
        for b in writes:
            b.last_w = [tok]
            b.readers = {}
        for b in reads:
            b.readers[sem[0].num] = tok
        return ins

    def barrier(self):
        toks = [(e.sem, e.count) for e in self.engs if e.count > 0]
        toks += [(s[0], s[1]) for s in self.dma_sems if s[1] > 0]
        for e in self.engs:
            self._wait(e, [t for t in toks if t[0].num != e.sem.num])
            if e.pend:
                e.log.append((e.pend, []))
                e.pend = []

    def simulate(self):
        sem = {}
        pc = {e.name: 0 for e in self.engs}
        progress = True
        while progress:
            progress = False
            for e in self.engs:
                while pc[e.name] < len(e.log):
                    waits, incs = e.log[pc[e.name]]
                    if all(sem.get(k_, 0) >= v for k_, v in waits):
                        for k_, a in incs:
                            sem[k_] = sem.get(k_, 0) + a
                        pc[e.name] += 1
                        progress = True
                    else:
                        break
        stuck = {e.name: (pc[e.name], len(e.log)) for e in self.engs if pc[e.name] < len(e.log)}
        if stuck:
            info = {}
            for e in self.engs:
                if pc[e.name] < len(e.log):
                    waits, _ = e.log[pc[e.name]]
                    info[e.name] = [(k_, v, sem.get(k_, 0)) for k_, v in waits if sem.get(k_, 0) < v]
            raise RuntimeError(f"DEADLOCK in sync graph: {stuck} {info} sems={ {e.name: e.sem.num for e in self.engs} }")
        return True

    def ev(self):
        mode = os.environ.get("K_EV", "dve")
        if mode == "act":
            return self.act
        if mode == "dve":
            return self.dve
        self.rr ^= 1
        return self.act if self.rr else self.dve


def copy_on(k, eng, out, in_, reads, writes, scale=None):
    if eng is k.act:
        if scale is None:
            return k.op(eng, lambda: eng.h.activation(out=out, in_=in_, func=AF.Copy), reads, writes)
        return k.op(eng, lambda: eng.h.activation(out=out, in_=in_, func=AF.Copy, scale=scale), reads, writes)
    if scale is None:
        return k.op(eng, lambda: eng.h.tensor_copy(out, in_), reads, writes)
    return k.op(eng, lambda: eng.h.tensor_scalar(out, in_, scale, None, ALU.mult), reads, writes)


def t5_bucket_np(rel):
    n = np.maximum(rel, 0)
    me = 16
    large = me + (np.log(np.maximum(n, 1).astype(np.float32) / me) / math.log(128 / me) * (32 - me)).astype(np.int32)
    large = np.minimum(large, 31)
    return np.where(n < me, n, large)


def w_in_perm():
    cols = []
    for j in range(4):
        cols += list(range(j * 64, j * 64 + 64)) + list(range(256 + j * 64, 256 + j * 64 + 64))
    cols += list(range(512, 768))
    P0 = 768
    cols += list(range(P0 + 1536, P0 + 1664))
    cols += list(range(P0 + 1664, P0 + 1792))
    for hp in range(4):
        for q in range(3):
            cols += list(range(P0 + q * 512 + hp * 128, P0 + q * 512 + hp * 128 + 128))
    return np.array(cols)


def host_consts():
    c = {}
    c["ident"] = np.eye(128, dtype=np.float32)
    s = np.arange(128)[:, None]
    t = np.arange(128)[None, :]
    su = (s < t).astype(np.float32)
    iu = (s <= t).astype(np.float32)
    sl = (s > t).astype(np.float32)
    c["rmask"] = np.concatenate([su, iu, sl], axis=1)
    kk = np.arange(128)[:, None]
    qq = np.arange(128)[None, :]
    m = np.zeros((128, 2, 128), np.float32)
    m[:, 0, :] = np.where(kk > qq, 0.0, -1e4)
    m[:, 1, :] = np.where(kk <= qq, 0.0, -1e4)
    c["amask"] = m
    rel = np.zeros((128, 2, 128), np.int64)
    rel[:, 0, :] = qq + 128 - kk
    rel[:, 1, :] = qq - kk
    c["_bucket"] = t5_bucket_np(rel)
    ind = np.zeros((128, 2), np.float32)
    ind[:64, 0] = 1.0
    ind[64:, 1] = 1.0
    c["ind2"] = ind
    blk = np.zeros((128, 128), np.float32)
    blk[:64, :64] = 1.0
    blk[64:, 64:] = 1.0
    c["blk"] = blk
    return c


def host_layout(inp):
    l = 0
    f = lambda a: np.ascontiguousarray(np.asarray(a, dtype=np.float32))
    hc = host_consts()
    d = {}
    perm = w_in_perm()
    d["w_in"] = f(inp["w_in"][l][:, perm])
    mix = np.asarray(inp["rwkv_shift_mix"][l], np.float32)
    mixp = mix[perm[768:] - 768]
    d["mixc"] = f(mixp.reshape(14, 128).T)
    d["g_pre"] = f(np.asarray(inp["norm_mix_pre"][l]).reshape(8, 128).T)
    d["g_ffn"] = f(np.asarray(inp["norm_ffn_pre"][l]).reshape(8, 128).T)
    d["g_post_b"] = f(np.broadcast_to(np.asarray(inp["norm_mix_post"][l])[None, :], (128, 1024)))
    d["g_fpost_b"] = f(np.broadcast_to(np.asarray(inp["norm_ffn_post"][l])[None, :], (128, 1024)))
    d["ident"] = hc["ident"]
    d["rmask"] = hc["rmask"]
    d["amask"] = hc["amask"]
    d["ind2"] = hc["ind2"]
    d["blk"] = hc["blk"]
    rb = np.asarray(inp["rel_bias"], np.float32)
    d["abias"] = f(rb[hc["_bucket"]].transpose(0, 1, 3, 2))
    d["sinks"] = f(np.asarray(inp["sinks"][l]).reshape(1, 8))
    col = lambda name: f(np.asarray(inp[name][l]).reshape(4, 128).T)
    d["rw_cols"] = f(np.concatenate([col("w0"), col("a0"), col("k_k"), col("k_a"),
                                     np.asarray(inp["r_k"][l], np.float32).reshape(4, 128).T], axis=1))
    d["lnx_b"] = f(np.stack([np.broadcast_to(np.asarray(inp["ln_x_g"][l])[None, :], (128, 512)),
                             np.broadcast_to(np.asarray(inp["ln_x_b"][l])[None, :], (128, 512))], axis=1))
    d["w_lora"] = f(np.concatenate([inp["w_decay_up"][l], inp["w_iclr_up"][l]], axis=0))
    d["w_gate"] = f(inp["w_gate_up"][l])
    d["w_out"] = f(inp["w_out"][l])
    wu = np.asarray(inp["w_ffn_up"][l], np.float32)
    d["w_up"] = f(wu.reshape(1024, 2, 32, 128).transpose(0, 2, 1, 3).reshape(1024, 8192))
    cw = np.asarray(inp["conv_w"][l], np.float32).reshape(3, 2, 32, 128)
    d["conv_w"] = f(cw.transpose(3, 2, 1, 0).reshape(128, 64, 3))
    cb = np.asarray(inp["conv_b"][l], np.float32).reshape(2, 32, 128)
    d["conv_b"] = f(cb.transpose(2, 1, 0).reshape(128, 64))
    d["w_down"] = f(inp["w_ffn_down"][l])
    return d


SHARED = ["w_in", "mixc", "g_pre", "g_ffn", "g_post_b", "g_fpost_b", "ident", "rmask", "amask", "ind2", "blk",
          "abias", "sinks", "rw_cols", "lnx_b", "w_lora", "w_gate", "w_out", "w_up", "conv_w", "conv_b", "w_down"]
SHAPES = {"w_in": [1024, 2560], "mixc": [128, 14], "g_pre": [128, 8], "g_ffn": [128, 8], "g_post_b": [128, 1024],
          "g_fpost_b": [128, 1024], "ident": [128, 128], "rmask": [128, 384], "amask": [128, 2, 128], "ind2": [128, 2],
          "blk": [128, 128], "abias": [128, 2, 8, 128], "sinks": [1, 8], "rw_cols": [128, 20], "lnx_b": [128, 2, 512],
          "w_lora": [128, 512], "w_gate": [128, 512], "w_out": [1024, 1024], "w_up": [1024, 8192],
          "conv_w": [128, 64, 3], "conv_b": [128, 64], "w_down": [4096, 1024]}

ARENA_BYTES = 206 * 1024


class Prog:
    def __init__(self, debug=()):
        self.debug = set(debug)
        nc = bass.Bass("TRN2", target_bir_lowering=False)
        self.nc = nc
        self.din = {n: nc.dram_tensor(n, SHAPES[n], F32, kind="ExternalInput").ap() for n in SHARED}
        self.x = nc.dram_tensor("x", [S, D], F32, kind="ExternalInput").ap()
        self.out = nc.dram_tensor("out", [S, D], F32, kind="ExternalOutput").ap()
        self.dbg = {}
        self.early_consts = False
        self.k = K(nc)

    def dbg_out(self, name, shape):
        t = self.nc.dram_tensor("dbg_" + name, shape, F32, kind="ExternalOutput").ap()
        self.dbg[name] = t
        return t

    def build(self, upto="all"):
        nc, k = self.nc, self.k
        self.upto = upto
        with nc.sbuf_tensor("arena", [128, ARENA_BYTES], U8) as arena, \
                nc.psum_tensor("ps0", [128, 512], F32) as p0, nc.psum_tensor("ps1", [128, 512], F32) as p1, \
                nc.psum_tensor("ps2", [128, 512], F32) as p2, nc.psum_tensor("ps3", [128, 512], F32) as p3, \
                nc.psum_tensor("ps4", [128, 512], F32) as p4, nc.psum_tensor("ps5", [128, 512], F32) as p5, \
                nc.psum_tensor("ps6", [128, 512], F32) as p6, nc.psum_tensor("ps7", [128, 512], F32) as p7:
            k.set_arena(arena, ARENA_BYTES)
            self.ps = [Buf(p[:, :], f"ps{i}", excl=True) for i, p in enumerate([p0, p1, p2, p3, p4, p5, p6, p7])]
            self.psi = 0
            self.sem_c = k.dma_sem("const")
            self.sem_o = k.dma_sem("out")
            self.sem_o2 = [self.sem_o, k.dma_sem("out1")]
            self.sem_dbg = k.dma_sem("dbg")
            self.early_consts = True
            self.consts()
            self.p3_consts()
            self.early_consts = False
            self.phase1()
            if upto == "p1":
                return self.finish()
            self.phase2()
            if upto == "p2":
                return self.finish()
            self.phase3()
            if upto.startswith("p3"):
                return self.finish()
            self.phase5()
            if upto == "p5":
                return self.finish()
            self.phase6()
            self.finish()

    def finish(self):
        self.k.barrier()
        self.k.simulate()
        return self.nc

    def bank(self):
        b = self.ps[self.psi]
        self.psi = (self.psi + 1) % 8
        return b

    def load_const(self, name, shape, dtype=F32, src=None, eng=None):
        k = self.k
        b = k.alloc(name, shape, dtype)
        src = self.din[name] if src is None else src
        eng = eng or (k.pool if dtype != F32 else (k.act if self.early_consts else k.sp))
        k.dma(eng, b[:], src, writes=[b], sem=self.sem_c)
        return b

    def consts(self):
        k, nc = self.k, self.nc
        self.identf = self.load_const("ident", [128])
        self.identb = self.load_const("identb", [128], BF16, src=self.din["ident"])
        self.g_pre = self.load_const("g_pre", [8])
        self.g_ffn = self.load_const("g_ffn", [8])
        self.eps = k.alloc("eps", [1], F32)
        k.op(k.pool, lambda: nc.gpsimd.memset(self.eps[:], 1e-6), [], [self.eps])
        self.zero1 = k.alloc("zero1", [1], F32)
        k.op(k.pool, lambda: nc.gpsimd.memset(self.zero1[:], 0.0), [], [self.zero1])
        self.A_tok = k.alloc("A_tok", [NB, 512], BF16, high=True)
        self.R_tok = k.alloc("R_tok", [NB, 512], BF16, high=True)
        self.h_mark = k.htop
        self.hT = k.alloc("hT", [8, S], BF16, high=True)
        self.h_mark_hT = k.htop
        self.sem_wpre = k.dma_sem("wpre")
        w_src = self.din["w_in"].rearrange("(kc p) n -> p kc n", p=128)
        self.wl = k.alloc("wl", [8, 256], BF16, high=True)
        self.h_mark_wl = k.htop
        self.wq = k.alloc("wq", [8, 768], BF16, high=True)
        self._wsrc = w_src

    def prefetch_qkv(self, after_tokens):
        k = self.k
        k._wait(k.pool, after_tokens)
        w_src = self._wsrc
        for h4 in range(2):
            k.dma(k.pool, self.wq[:, h4 * 4:(h4 + 1) * 4, :], w_src[:, h4 * 4:(h4 + 1) * 4, 0:768], writes=[self.wq], sem=self.sem_wpre)
        for h4 in range(2):
            k.dma(k.pool, self.wl[:, h4 * 4:(h4 + 1) * 4, :], w_src[:, h4 * 4:(h4 + 1) * 4, 768:1024], writes=[self.wl], sem=self.sem_wpre)

    def norm_T(self, src_fn, g_cols, dst, ngroups, xt_ring, xn_ring, sq, ss, rs, t_base=0):
        for _ in self.norm_T_gen(src_fn, g_cols, dst, ngroups, xt_ring, xn_ring, sq, ss, rs):
            pass

    def norm_T_gen(self, src_fn, g_cols, dst, ngroups, xt_ring, xn_ring, sq, ss, rs, G=None, pipelined=False, act_kc=None):
        k, nc = self.k, self.nc
        G = G or len(xn_ring)
        NX = len(xn_ring)

        def part_a(grp):
            for i in range(G):
                t = grp * G + i
                xt = xt_ring[t % len(xt_ring)]
                src_fn(t, xt)
                k.op(k.act, lambda: nc.scalar.activation(out=sq[:], in_=xt[:], func=AF.Square, accum_out=ss[:, t:t + 1]),
                     [xt], [sq, ss])
            sl = slice(grp * G, grp * G + G)
            k.op(k.dve, lambda: nc.vector.tensor_scalar(rs[:, sl], ss[:, sl], 1.0 / D, 1e-6, ALU.mult, ALU.add), [ss], [rs])
            k.op(k.act, lambda: nc.scalar.activation(out=rs[:, sl], in_=rs[:, sl], func=AF.Sqrt), [rs], [rs])
            k.op(k.dve, lambda: nc.vector.reciprocal(rs[:, sl], rs[:, sl]), [rs], [rs])
            for i in range(G):
                t = grp * G + i
                xt = xt_ring[t % len(xt_ring)]
                xn = xn_ring[t % NX]
                k.op(k.dve, lambda: nc.vector.tensor_scalar(xn[:], xt[:], rs[:, t:t + 1], None, ALU.mult), [xt, rs], [xn])

        def part_b(grp):
            for kc in range(8):
                bk = self.bank()
                for i in range(G):
                    xn = xn_ring[(grp * G + i) % NX]
                    k.op(k.pe, lambda: nc.tensor.transpose(out=bk[:, i * 128:(i + 1) * 128], in_=xn[:, kc * 128:(kc + 1) * 128],
                                                           identity=self.identf[:]), [xn, self.identf], [bk], inc=(i == G - 1))
                ee = k.ev() if act_kc is None else (k.act if kc in act_kc else k.dve)
                copy_on(k, ee, dst[:, kc, grp * G * 128:(grp + 1) * G * 128], bk[:, 0:G * 128], [bk, g_cols], [dst], scale=g_cols[:, kc:kc + 1])

        if not pipelined:
            for grp in range(ngroups):
                part_a(grp)
                yield "A"
                part_b(grp)
                yield "B"
        else:
            part_a(0)
            for grp in range(ngroups):
                if grp + 1 < ngroups:
                    part_a(grp + 1)
                    yield "A"
                part_b(grp)
                yield "B"

    def phase1(self):
        k, nc = self.k, self.nc
        mk = k.mark()
        xt_ring = [k.alloc(f"xt{i}", [D], F32) for i in range(8)]
        xn_ring = [k.alloc(f"xn{i}", [D], F32) for i in range(8)]
        sq = k.alloc("sq", [D], F32)
        ss = k.alloc("ss", [NB], F32)
        rs = k.alloc("rs", [NB], F32)
        sems = [k.dma_sem(f"x{i}") for i in range(8)]

        def src(t, xt):
            k.dma(k.sp, xt[:], self.x[t * 128:(t + 1) * 128, :], writes=[xt], sem=sems[t % 8])
            if t == 7:
                self.prefetch_qkv(list(xt.last_w))
        akc = {"2": (3, 7), "0": (), "4": None}[os.environ.get("K_P1A", "0")]
        for _ in self.norm_T_gen(src, self.g_pre, self.hT, 4, xt_ring, xn_ring, sq, ss, rs, G=4, pipelined=True, act_kc=akc):
            pass
        self.x_sems = sems
        if "hT" in self.debug:
            self.dump("hT", self.hT, [8, S])
        k.release(mk)

    def dump(self, name, buf, shape, part=128):
        k, nc = self.k, self.nc
        n = int(np.prod(shape))
        o = self.dbg_out(name, [128, n])
        mk = k.mark()
        CH = 2048
        st = k.alloc("dbg_st", [CH], F32)
        flat = buf.ap
        if len(shape) == 2:
            flat = flat.rearrange("p a b -> p (a b)")
        elif len(shape) == 3:
            flat = flat.rearrange("p a b c -> p (a b c)")
        elif len(shape) == 4:
            flat = flat.rearrange("p a b c d -> p (a b c d)")
        for c0 in range(0, n, CH):
            w = min(CH, n - c0)
            k.op(k.dve, lambda: nc.vector.tensor_copy(st[:, 0:w], flat[:, c0:c0 + w]), [buf], [st])
            k.dma(k.sp, o[:, c0:c0 + w], st[:, 0:w], reads=[st], sem=self.sem_dbg)
        k.release(mk)

    def proj_fm(self, w, wcols, dst_fn, ntq=4, M=128, tq_list=None):
        k, nc = self.k, self.nc
        for tq in (tq_list if tq_list is not None else range(ntq)):
            bk = self.bank()
            for kc in range(8):
                k.op(k.pe, lambda: nc.tensor.matmul(bk[0:M, :], w[:, kc, wcols], self.hT[:, kc, tq * 512:(tq + 1) * 512],
                                                    start=(kc == 0), stop=(kc == 7)),
                     [w, self.hT], [bk], inc=(kc == 7))
            dst_fn(tq, bk)

    def phase2(self):
        k, nc = self.k, self.nc
        mk = k.mark()
        wq = self.wq
        QT = k.alloc("QT", [4, S], BF16)
        KT = k.alloc("KT", [S], BF16)
        Vx = k.alloc("Vx", [NB, 2, 66], BF16)
        k.op(k.pool, lambda: nc.gpsimd.memset(Vx[:], 1.0), [], [Vx])
        biasm = self.load_const("abias", [2, 8, 128])
        amask = self.load_const("amask", [2, 128])
        sinks = self.load_const("sinks", [8], src=self.din["sinks"].partition_broadcast(128))
        for w_ in range(2):
            k.op(k.dve, lambda: nc.vector.tensor_tensor(out=biasm[:, w_], in0=biasm[:, w_],
                                                        in1=amask[:, w_:w_ + 1, :].to_broadcast([128, 8, 128]), op=ALU.add),
                 [biasm, amask], [biasm])
            k.op(k.dve, lambda: nc.vector.tensor_tensor(out=biasm[:, w_], in0=biasm[:, w_],
                                                        in1=sinks[:, :].unsqueeze(2).to_broadcast([128, 8, 128]), op=ALU.subtract),
                 [biasm, sinks], [biasm])
        for j in range(4):
            self.proj_fm(wq, slice(j * 128, (j + 1) * 128),
                         lambda tq, bk: copy_on(k, k.ev(), QT[:, j, tq * 512:(tq + 1) * 512], bk[:, :], [bk], [QT]))
        self.proj_fm(wq, slice(512, 640), lambda tq, bk: copy_on(k, k.ev(), KT[:, tq * 512:(tq + 1) * 512], bk[:, :], [bk], [KT]))
        for n in range(NB):
            bk = self.bank()
            for kc in range(8):
                k.op(k.pe, lambda: nc.tensor.matmul(bk[:, 0:128], self.hT[:, kc, n * 128:(n + 1) * 128], wq[:, kc, 640:768],
                                                    start=(kc == 0), stop=(kc == 7)), [wq, self.hT], [bk], inc=(kc == 7))
            copy_on(k, k.ev(), Vx[:, n, :, 0:64], bk[:, 0:128].rearrange("p (h d) -> p h d", d=64), [bk], [Vx])
        if "QT" in self.debug:
            self.dump("QT", QT, [4, S])
            self.dump("KT", KT, [S])
            self.dump("Vx", Vx, [NB, 2, 66])
        sp_ring = [k.alloc(f"sp{i}", [512], F32) for i in range(4)]
        pt_ring = [k.alloc(f"pt{i}", [4, 128], BF16) for i in range(4)]
        den = [k.alloc(f"den{i}", [4], F32) for i in range(2)]
        st_ = {"ri": 0}

        def att_a(n, ph):
            rows = slice(ph * 64, ph * 64 + 64)
            pts = {}
            for w_ in ((1,) if n == 0 else (0, 1)):
                kb = n - 1 + w_
                bk = self.bank()
                for j in range(4):
                    k.op(k.pe, lambda: nc.tensor.matmul(bk[:, j * 128:(j + 1) * 128], KT[rows, kb * 128:(kb + 1) * 128],
                                                        QT[rows, j, n * 128:(n + 1) * 128], start=True, stop=True),
                         [KT, QT], [bk], inc=(j == 3))
                sp = sp_ring[st_["ri"] % 4]
                pt = pt_ring[st_["ri"] % 4]
                st_["ri"] += 1
                k.op(k.dve, lambda: nc.vector.scalar_tensor_tensor(
                    out=sp[:], in0=bk[:, :], scalar=0.125,
                    in1=biasm[:, w_, 4 * ph:4 * ph + 4, :].rearrange("p h q -> p (h q)"), op0=ALU.mult, op1=ALU.add),
                    [bk, biasm], [sp])
                k.op(k.act, lambda: nc.scalar.activation(out=pt[:].rearrange("p h q -> p (h q)"), in_=sp[:], func=AF.Exp),
                     [sp], [pt])
                pts[w_] = (pt, kb)
            return pts

        def att_b(n, ph, pts):
            bo = self.bank()
            bov = bo[:, 0:264].rearrange("p (h d) -> p h d", d=66)
            ws = sorted(pts.keys())
            for j in range(4):
                for wi, w_ in enumerate(ws):
                    pt, kb = pts[w_]
                    k.op(k.pe, lambda: nc.tensor.matmul(bov[:, j, 0:65], pt[:, j, :], Vx[:, kb, ph, 0:65],
                                                        start=(wi == 0), stop=(wi == len(ws) - 1)),
                         [pt, Vx], [bo], inc=(j == 3 and wi == len(ws) - 1))
            dn = den[(n * 2 + ph) % 2]
            k.op(k.dve, lambda: nc.vector.tensor_scalar(dn[:], bov[:, :, 64], 1.0, None, ALU.add), [bo], [dn])
            k.op(k.dve, lambda: nc.vector.reciprocal(dn[:], dn[:]), [dn], [dn])
            k.op(k.dve, lambda: nc.vector.tensor_tensor(
                out=self.A_tok[:, n, ph * 256:(ph + 1) * 256].rearrange("p (h d) -> p h d", d=64),
                in0=bov[:, :, 0:64], in1=dn[:, :].unsqueeze(2).to_broadcast([128, 4, 64]), op=ALU.mult),
                [bo, dn], [self.A_tok])

        its = [(n, ph) for n in range(NB) for ph in range(2)]
        prev = None
        for it in its:
            pts = att_a(*it)
            if prev is not None:
                att_b(*prev)
            prev = (it[0], it[1], pts)
        att_b(*prev)
        if "A_tok" in self.debug:
            self.dump("A_tok", self.A_tok, [NB, 512])
        k.release(mk)
        k.release_high(self.h_mark_wl)

    def p3_consts(self):
        k, nc = self.k, self.nc
        self.mk_p3c = k.mark()
        WL = self.load_const("w_lora", [512], BF16)
        Wg = self.load_const("w_gate", [512], BF16)
        self.mixc = self.load_const("mixc", [14])
        rwc = self.load_const("rw_cols", [20])
        lnx = self.load_const("lnx_b", [2, 512])
        rmask = self.load_const("rmask", [384], BF16)
        ind2 = self.load_const("ind2", [2], BF16)
        blk = self.load_const("blkb", [128], BF16, src=self.din["blk"])
        self.omix = k.alloc("omix", [14], F32)
        k.op(k.dve, lambda: nc.vector.tensor_scalar(self.omix[:], self.mixc[:], -1.0, 1.0, ALU.mult, ALU.add), [self.mixc], [self.omix])
        tiny = k.alloc("tiny", [1], F32)
        k.op(k.pool, lambda: nc.gpsimd.memset(tiny[:], 1e-30), [], [tiny])
        omka = k.alloc("omka", [4], F32)
        k.op(k.dve, lambda: nc.vector.tensor_scalar(omka[:], rwc[:, 12:16], -1.0, 1.0, ALU.mult, ALU.add), [rwc], [omka])
        self.C3 = dict(WL=WL, Wg=Wg, rwc=rwc, lnx=lnx, rmask=rmask, ind2=ind2, blk=blk, omka=omka, tiny=tiny)

    def shift_chunk(self, w, wcols, mixcol, F0, F1, dst_fn):
        k, nc = self.k, self.nc
        k.op(k.pool, lambda: nc.gpsimd.memset(F0[:, 0:1], 0.0), [], [F0])
        self.proj_fm(w, wcols, lambda tq, bk: copy_on(k, k.ev(), F0[:, 1 + tq * 512:1 + (tq + 1) * 512], bk[:, :], [bk], [F0]))
        k.op(k.act, lambda: nc.scalar.activation(out=F1[:, 0:S], in_=F0[:, 1:S + 1], func=AF.Copy,
                                                 scale=self.omix[:, mixcol:mixcol + 1]), [F0, self.omix], [F1])
        dst_fn(F0[:, 0:S], self.mixc[:, mixcol:mixcol + 1])

    def phase3(self):
        k, nc = self.k, self.nc
        CW = 0.6065306597126334
        mk = k.mark()
        sem_w = k.dma_sem("wl")
        w_src = self.din["w_in"].rearrange("(kc p) n -> p kc n", p=128)
        C3 = self.C3
        WL, Wg, rwc, lnx, rmask, ind2, blk, omka, tiny = (C3[n_] for n_ in ("WL", "Wg", "rwc", "lnx", "rmask", "ind2", "blk", "omka", "tiny"))
        L0T = k.alloc("L0T", [S], BF16)
        L1T = k.alloc("L1T", [S], BF16)
        mk_hp = k.mark()
        wl = self.wl

        F = [k.alloc(f"F{i}", [S + 1], F32) for i in range(2)]
        F2 = k.alloc("F2s", [S], F32)
        self.shift_chunk(wl, slice(0, 128), 0, F[0], F[1],
                         lambda prev, mix: k.op(k.dve, lambda: nc.vector.scalar_tensor_tensor(
                             out=F2[:], in0=prev, scalar=mix, in1=F[1][:, 0:S], op0=ALU.mult, op1=ALU.add),
                             [F[0], F[1], self.mixc], [F2]))
        k.op(k.act, lambda: nc.scalar.activation(out=L0T[0:64, :], in_=F2[0:64, :], func=AF.Tanh), [F2], [L0T])
        k.op(k.act, lambda: nc.scalar.activation(out=L0T[64:128, :], in_=F2[64:128, :], func=AF.Copy), [F2], [L0T])
        self.shift_chunk(wl, slice(128, 256), 1, F[0], F[1],
                         lambda prev, mix: k.op(k.dve, lambda: nc.vector.scalar_tensor_tensor(
                             out=F2[:], in0=prev, scalar=mix, in1=F[1][:, 0:S], op0=ALU.mult, op1=ALU.add),
                             [F[0], F[1], self.mixc], [F2]))
        k.op(k.act, lambda: nc.scalar.activation(out=L1T[:], in_=F2[:], func=AF.Sigmoid), [F2], [L1T])
        k.release(mk_hp)
        k.release_high(self.h_mark_hT)
        if self.upto == "p3a":
            if "L0T" in self.debug:
                self.dump("L0T", L0T, [S]); self.dump("L1T", L1T, [S])
            return

        ps = self.ps
        msk_su_iu = rmask[:, 0:256]
        msk_sl = rmask[:, 256:384]
        v3 = lambda ap: ap.rearrange("p (c n) -> p c n", n=128)

        NQ = int(os.environ.get("K_NQ", "4"))

        def prep_bufs():
            wr = k.alloc("wr", [8, 384], BF16)
            Fb = [k.alloc(f"F{i}", [S + 1], F32) for i in range(6)]
            FH = [[Buf(f_.ap, f"{f_.name}h{h}") for h in range(NQ)] for f_ in Fb]
            for f_, fh_ in zip(Fb, FH):
                for c_ in fh_:
                    c_.readers = dict(f_.readers)
            return wr, Fb, FH

        def prep(hp, P, wr, Fb, FH):
            c4 = slice(hp * 128, (hp + 1) * 128)
            col = lambda q: rwc[:, 4 * q + hp:4 * q + hp + 1]
            ar, kT, bT, prod, vs, gC = P["ar"], P["kT"], P["bT"], P["prod"], P["vs"], P["gC"]
            for h4 in range(2):
                k.dma(k.pool, wr[:, h4 * 4:(h4 + 1) * 4, :], w_src[:, h4 * 4:(h4 + 1) * 4, 1024 + hp * 384:1024 + (hp + 1) * 384], writes=[wr], sem=sem_w)
            HS = S // NQ
            HC = NB // NQ
            TQ = 4 // NQ

            def proj_half(h, wcols, dst_fn, lhs_rows=None):
                for tq in range(h * TQ, (h + 1) * TQ):
                    bk = self.bank()
                    for kc in range(8):
                        k.op(k.pe, lambda: nc.tensor.matmul(bk[:, :], wr[:, kc, wcols], self.hT[:, kc, tq * 512:(tq + 1) * 512],
                                                            start=(kc == 0), stop=(kc == 7)), [wr, self.hT], [bk], inc=(kc == 7))
                    dst_fn(tq, bk)

            def shift_half(h, wcols, mixcol, iraw, it1, out_fn):
                Fr, Ft = Fb[iraw], Fb[it1]
                t0, t1 = h * HS, (h + 1) * HS
                if h == 0:
                    k.op(k.pool, lambda: nc.gpsimd.memset(Fr[:, 0:1], 0.0), [], [FH[iraw][0]])
                proj_half(h, wcols, lambda tq, bk: copy_on(k, k.act if (h % 2 == 0 or (h == 1 and os.environ.get('K_PQ', '1') == '1')) else k.dve, Fr[:, 1 + tq * 512:1 + (tq + 1) * 512], bk[:, :], [bk], [FH[iraw][h]]))
                k.op(k.act, lambda: nc.scalar.activation(out=Ft[:, t0:t1], in_=Fr[:, 1 + t0:1 + t1], func=AF.Copy,
                                                         scale=self.omix[:, mixcol:mixcol + 1]), [FH[iraw][h], self.omix], [FH[it1][h]])
                out_fn(Fr[:, t0:t1], self.mixc[:, mixcol:mixcol + 1], [FH[iraw][max(h - 1, 0)], FH[iraw][h], FH[it1][h], self.mixc])

            def v3h(ap):
                return ap.rearrange("p (c n) -> p c n", n=128)

            def steps(h):
                t0, t1 = h * HS, (h + 1) * HS
                T = slice(t0, t1)
                T1 = slice(t0 + 1, t1 + 1)
                C = slice(h * HC, (h + 1) * HC)
                F0, F1, F2, F3, F4, F5 = Fb
                B0, B1, B2, B3, B4, B5 = [FH[i][h] for i in range(6)]
                sqb_ap = F0.ap.bitcast(BF16)
                blkb = blk
                for tq in range(h * TQ, (h + 1) * TQ):
                    bk = self.bank()
                    k.op(k.pe, lambda: nc.tensor.matmul(bk[:, :], WL[64:128, c4], L0T[64:128, tq * 512:(tq + 1) * 512], start=True, stop=True),
                         [WL, L0T], [bk])
                    k.op(k.act, lambda: nc.scalar.activation(out=F3[:, tq * 512:(tq + 1) * 512], in_=bk[:, :], func=AF.Sigmoid, bias=col(1)),
                         [bk, rwc], [B3])
                yield
                for tq in range(h * TQ, (h + 1) * TQ):
                    bk = self.bank()
                    k.op(k.pe, lambda: nc.tensor.matmul(bk[:, :], WL[0:64, c4], L0T[0:64, tq * 512:(tq + 1) * 512], start=True, stop=True),
                         [WL, L0T], [bk])
                    k.op(k.act, lambda: nc.scalar.activation(out=F5[:, tq * 512:(tq + 1) * 512], in_=bk[:, :], func=AF.Sigmoid, bias=col(0)),
                         [bk, rwc], [B5])
                yield
                if h == 0:
                    k.op(k.pool, lambda: nc.gpsimd.memset(F4[:, 0:1], 0.0), [], [FH[4][0]])
                    k.op(k.dve, lambda: nc.vector.tensor_tensor_scan(out=F4[:, T1], data0=self.zero1[:, 0:1].to_broadcast([128, HS]),
                                                                     data1=F5[:, T], initial=0.0, op0=ALU.add, op1=ALU.add),
                         [B5, self.zero1, FH[4][0]], [B4])
                else:
                    k.op(k.dve, lambda: nc.vector.tensor_tensor_scan(out=F4[:, T1], data0=self.zero1[:, 0:1].to_broadcast([128, HS]),
                                                                     data1=F5[:, T], initial=F4[:, t0:t0 + 1], op0=ALU.add, op1=ALU.add),
                         [B5, self.zero1, FH[4][h - 1]], [B4])
                yield
                shift_half(h, slice(128, 256), 2 + 3 * hp + 1, 0, 1,
                           lambda prev, mix, rd: k.op(k.dve, lambda: nc.vector.scalar_tensor_tensor(
                               out=F2[:, T], in0=prev, scalar=mix, in1=F1[:, T], op0=ALU.mult, op1=ALU.add), rd, [B2]))
                yield
                k.op(k.act, lambda: nc.scalar.activation(out=F1[:, T], in_=F2[:, T], func=AF.Copy, scale=col(2)), [B2, rwc], [B1])
                k.op(k.act, lambda: nc.scalar.activation(out=sqb_ap[:, T], in_=F1[:, T], func=AF.Square), [B1], list(FH[0]))
                yield
                for tq in range(h * TQ, (h + 1) * TQ):
                    bk = self.bank()
                    k.op(k.pe, lambda: nc.tensor.matmul(bk[:, :], blkb[:], sqb_ap[:, tq * 512:(tq + 1) * 512], start=True, stop=True), [blkb] + list(FH[0]), [bk])
                    k.op(k.act, lambda: nc.scalar.activation(out=F5[:, tq * 512:(tq + 1) * 512], in_=bk[:, :], func=AF.Ln, bias=tiny[:, 0:1]),
                         [bk, tiny], [B5])
                yield
                k.op(k.act, lambda: nc.scalar.activation(out=F5[:, T], in_=F5[:, T], func=AF.Exp, scale=-0.5), [B5], [B5])
                yield
                k.op(k.dve, lambda: nc.vector.tensor_tensor(out=F1[:, T], in0=F1[:, T], in1=F5[:, T], op=ALU.mult), [B1, B5], [B1])
                k.op(k.act, lambda: nc.scalar.activation(out=F0[:, T], in_=F3[:, T], func=AF.Identity, scale=col(3), bias=omka[:, hp:hp + 1]),
                     [B3, rwc, omka], [B0])
                yield
                k.op(k.dve, lambda: nc.vector.tensor_tensor(out=F2[:, T], in0=F2[:, T], in1=F0[:, T], op=ALU.mult), [B2, B0], [B2])
                k.op(k.dve, lambda: nc.vector.tensor_tensor(out=F3[:, T], in0=F3[:, T], in1=F1[:, T], op=ALU.mult), [B3, B1], [B3])
                yield
                base = v3h(F4[:, T])[:, :, 0:1].to_broadcast([128, HC, 128])
                rb = [FH[4][max(h - 1, 0)], B4]
                k.op(k.dve, lambda: nc.vector.tensor_tensor(out=v3h(F0[:, T]), in0=v3h(F4[:, T1]), in1=base, op=ALU.subtract), rb, [B0])
                yield
                k.op(k.act, lambda: nc.scalar.activation(out=F5[:, T], in_=F0[:, T], func=AF.Exp, scale=CW), [B0], [B5])
                yield
                k.op(k.dve, lambda: nc.vector.tensor_tensor(out=kT[:, T], in0=F2[:, T], in1=F5[:, T], op=ALU.mult), [B2, B5], [kT])
                k.op(k.dve, lambda: nc.vector.tensor_tensor(out=bT[:, T], in0=F3[:, T], in1=F5[:, T], op=ALU.mult), [B3, B5], [bT])
                yield
                k.op(k.act, lambda: nc.scalar.activation(out=F5[:, T], in_=F0[:, T], func=AF.Exp, scale=-CW), [B0], [B5])
                k.op(k.dve, lambda: nc.vector.tensor_copy(gC[:, C], v3h(F5[:, T])[:, :, 127]), [B5], [gC])
                yield
                shift_half(h, slice(0, 128), 2 + 3 * hp + 0, 3, 0,
                           lambda prev, mix, rd: k.op(k.dve, lambda: nc.vector.scalar_tensor_tensor(
                               out=F0[:, T], in0=prev, scalar=mix, in1=F0[:, T], op0=ALU.mult, op1=ALU.add), rd, [B0]))
                yield
                k.op(k.dve, lambda: nc.vector.tensor_tensor(out=ar[:, 1, T], in0=F0[:, T], in1=F5[:, T], op=ALU.mult),
                     [B0, B5], [ar])
                k.op(k.dve, lambda: nc.vector.scalar_tensor_tensor(out=prod[:, T], in0=F0[:, T], scalar=col(4), in1=F2[:, T],
                                                                   op0=ALU.mult, op1=ALU.mult), [B0, B2, rwc], [prod])
                yield
                k.op(k.dve, lambda: nc.vector.tensor_tensor(out=v3h(F0[:, T]), in0=v3h(F4[:, T]), in1=base, op=ALU.subtract), rb, [B0])
                yield
                k.op(k.act, lambda: nc.scalar.activation(out=F5[:, T], in_=F0[:, T], func=AF.Exp, scale=-CW), [B0], [B5])
                yield
                k.op(k.dve, lambda: nc.vector.scalar_tensor_tensor(out=ar[:, 0, T], in0=F1[:, T], scalar=-1.0, in1=F5[:, T],
                                                                   op0=ALU.mult, op1=ALU.mult), [B1, B5], [ar])
                yield
                shift_half(h, slice(256, 384), 2 + 3 * hp + 2, 3, 0,
                           lambda prev, mix, rd: k.op(k.dve, lambda: nc.vector.scalar_tensor_tensor(
                               out=vs[:, T], in0=prev, scalar=mix, in1=F0[:, T], op0=ALU.mult, op1=ALU.add), rd, [vs]))

            gens = [steps(h_) for h_ in range(NQ)]
            while gens:
                for g_ in list(gens):
                    try:
                        next(g_)
                    except StopIteration:
                        gens.remove(g_)

        def transposes(P):
            kT, bT, vs, tokm = P["kT"], P["bT"], P["vs"], P["tokm"]
            for c in range(0, NB, 2):
                bk = self.bank()
                pb = bk.ap.bitcast(BF16)
                for cc in range(2):
                    cs_ = slice((c + cc) * 128, (c + cc + 1) * 128)
                    for i, srcb in enumerate((kT, bT, vs)):
                        o_ = (cc * 3 + i) * 128
                        k.op(k.pe, lambda: nc.tensor.transpose(out=pb[:, o_:o_ + 128], in_=srcb[:, cs_], identity=self.identb[:]),
                             [srcb, self.identb], [bk], inc=(cc == 1 and i == 2))
                copy_on(k, k.ev(), tokm[:, c:c + 2].rearrange("p c a b -> p (c a b)"), pb[:, 0:768], [bk], [tokm])

        FILL = int(os.environ.get("K_FILL", "0"))
        fstate = {"first": True}

        def fill(n):
            for _ in range(n):
                if fstate["first"]:
                    k.op(k.pe, lambda: nc.tensor.matmul(ps[1][:, 0:128], self.identb[:], self.identb[:], start=True, stop=True),
                         [self.identb], [ps[1]])
                    fstate["first"] = False
                else:
                    k.op(k.pe, lambda: nc.tensor.matmul(ps[1][:, 0:128], self.identb[:], self.identb[:], start=True, stop=True), [], [], inc=False)

        def pre(c, Ps):
            par = c % 2
            heads = [(P, h) for P in Ps for h in range(2)]
            cs_ = slice(c * 128, (c + 1) * 128)
            eng_of = lambda hi: (k.act, k.act) if hi % 2 == 0 else (k.dve, k.dve)
            for hi, (P, h) in enumerate(heads):
                rows = slice(h * 64, h * 64 + 64)
                ar, kT, bT = P["ar"], P["kT"], P["bT"]
                bD, bB = ps[2 + hi], ps[0]
                bo = (hi % 2) * 256
                for w_ in range(2):
                    k.op(k.pe, lambda: nc.tensor.matmul(bD[:, w_ * 128:(w_ + 1) * 128], bT[rows, cs_], ar[rows, w_, cs_], start=True, stop=True),
                         [bT, ar], [bD], inc=False)
                k.op(k.pe, lambda: nc.tensor.matmul(bD[:, 256:384], ar[rows, 0, cs_], bT[rows, cs_], start=True, stop=True), [bT, ar], [bD])
                for w_ in range(2):
                    k.op(k.pe, lambda: nc.tensor.matmul(bB[:, bo + w_ * 128:bo + (w_ + 1) * 128], kT[rows, cs_], ar[rows, w_, cs_], start=True, stop=True),
                         [kT, ar], [bB], inc=(w_ == 1))
                fill(FILL)
                m1, m2, X0 = P["M1"][h][par], P["M2"][h][par], P["Xb"][h][0]
                k.op(k.dve, lambda: nc.vector.tensor_tensor(out=m1[:].rearrange("p a b -> p (a b)"), in0=bD[:, 0:256], in1=msk_su_iu, op=ALU.mult),
                     [bD, rmask], [m1])
                k.op(k.dve, lambda: nc.vector.tensor_tensor(out=X0[:], in0=bD[:, 256:384], in1=msk_sl, op=ALU.mult), [bD, rmask], [X0])
                k.op(k.dve, lambda: nc.vector.tensor_tensor(out=m2[:].rearrange("p a b -> p (a b)"), in0=bB[:, bo:bo + 256], in1=msk_su_iu, op=ALU.mult),
                     [bB, rmask], [m2])
            yield
            for hi, (P, h) in enumerate(heads):
                bD = ps[2 + hi]
                m1, X0 = P["M1"][h][par], P["Xb"][h][0]
                yt1 = P["YT"][h][0]
                e_big, e_small = eng_of(hi)
                k.op(k.pe, lambda: nc.tensor.matmul(bD[:, 0:128], X0[:], m1[:, 0, :], start=True, stop=True), [X0, m1], [bD], inc=False)
                k.op(k.pe, lambda: nc.tensor.matmul(bD[:, 256:384], m1[:, 0, :], X0[:], start=True, stop=True), [X0, m1], [bD])
                fill(FILL)
                copy_on(k, e_big, yt1[:, 0, :], bD[:, 0:128], [bD], [yt1])
                k.op(k.dve, lambda: nc.vector.tensor_tensor(out=yt1[:, 1, :], in0=m1[:, 0, :], in1=self.identb[:], op=ALU.add), [m1, self.identb], [yt1])
                copy_on(k, e_small, P["Xb"][h][1][:], bD[:, 256:384], [bD], [P["Xb"][h][1]])
            xi, yi = 1, 0
            p_ = 2
            while p_ < 128:
                yield
                last = (p_ * 2 >= 128)
                for hi, (P, h) in enumerate(heads):
                    bD = ps[2 + hi]
                    Xp, ytp = P["Xb"][h][xi], P["YT"][h][yi]
                    ytn = P["YT"][h][(yi + 1) % 3]
                    e_big, e_small = eng_of(hi)
                    if not last:
                        need_y = (p_ * 4 < 128)
                        if need_y:
                            k.op(k.pe, lambda: nc.tensor.matmul(bD[:, 0:128], Xp[:], ytp[:, 0, :], start=True, stop=True), [Xp, ytp], [bD], inc=False)
                        k.op(k.pe, lambda: nc.tensor.matmul(bD[:, 128:256], Xp[:], ytp[:, 1, :], start=True, stop=False), [Xp, ytp], [bD], inc=False)
                        k.op(k.pe, lambda: nc.tensor.matmul(bD[:, 128:256], self.identb[:], ytp[:, 1, :], start=False, stop=True),
                             [self.identb, ytp], [bD], inc=False)
                        k.op(k.pe, lambda: nc.tensor.matmul(bD[:, 256:384], ytp[:, 0, :], Xp[:], start=True, stop=True), [Xp, ytp], [bD])
                        fill(FILL)
                        if need_y:
                            copy_on(k, e_big, ytn[:].rearrange("p a b -> p (a b)"), bD[:, 0:256], [bD], [ytn])
                        else:
                            copy_on(k, e_big, ytn[:, 1, :], bD[:, 128:256], [bD], [ytn])
                        copy_on(k, e_small, P["Xb"][h][1 - xi][:], bD[:, 256:384], [bD], [P["Xb"][h][1 - xi]])
                    else:
                        tf = P["TTf"][h][par]
                        k.op(k.pe, lambda: nc.tensor.matmul(bD[:, 128:256], Xp[:], ytp[:, 1, :], start=True, stop=False), [Xp, ytp], [bD], inc=False)
                        k.op(k.pe, lambda: nc.tensor.matmul(bD[:, 128:256], self.identb[:], ytp[:, 1, :], start=False, stop=True),
                             [self.identb, ytp], [bD])
                        fill(FILL)
                        copy_on(k, e_big, tf[:], bD[:, 128:256], [bD], [tf])
                xi = 1 - xi
                yi = (yi + 1) % 3
                p_ *= 2

        def seq(c, Ps):
            par = c % 2
            hr = lambda h: slice(h * 64, h * 64 + 64)
            for q, P in enumerate(Ps):
                bS = ps[6 + q]
                for h in range(2):
                    k.op(k.pe, lambda: nc.tensor.matmul(bS[:, hr(h)], P["M2"][h][par][:, 0, :], P["tokm"][:, c, 2, hr(h)], start=True, stop=False),
                         [P["M2"][h][par], P["tokm"]], [bS], inc=False)
                    k.op(k.pe, lambda: nc.tensor.matmul(bS[:, hr(h)], P["ar"][hr(h), 0, c * 128:(c + 1) * 128], P["Hb"][hr(h), :], start=False, stop=True),
                         [P["ar"], P["Hb"]], [bS], inc=(h == 1))
            for q, P in enumerate(Ps):
                bS = ps[6 + q]
                copy_on(k, k.act if q == 0 else k.dve, P["Zs"][:], bS[:, 0:128], [bS], [P["Zs"]])
            yield
            for q, P in enumerate(Ps):
                bS = ps[6 + q]
                for h in range(2):
                    k.op(k.pe, lambda: nc.tensor.matmul(bS[:, 128 + h * 64:192 + h * 64], P["TTf"][h][par][:], P["Zs"][:, hr(h)], start=True, stop=True),
                         [P["TTf"][h][par], P["Zs"]], [bS], inc=(h == 1))
            for q, P in enumerate(Ps):
                bS = ps[6 + q]
                copy_on(k, k.act if q == 0 else k.dve, P["Us"][:], bS[:, 128:256], [bS], [P["Us"]])
            yield
            for q, P in enumerate(Ps):
                bS = ps[6 + q]
                bO = ps[1]
                for h in range(2):
                    oc = slice(q * 128 + h * 64, q * 128 + h * 64 + 64)
                    k.op(k.pe, lambda: nc.tensor.matmul(bO[:, oc], P["M2"][h][par][:, 1, :], P["tokm"][:, c, 2, hr(h)], start=True, stop=False),
                         [P["M2"][h][par], P["tokm"]], [bO], inc=False)
                    k.op(k.pe, lambda: nc.tensor.matmul(bO[:, oc], P["ar"][hr(h), 1, c * 128:(c + 1) * 128], P["Hb"][hr(h), :], start=False, stop=False),
                         [P["ar"], P["Hb"]], [bO], inc=False)
                    k.op(k.pe, lambda: nc.tensor.matmul(bO[:, oc], P["M1"][h][par][:, 1, :], P["Us"][:, hr(h)], start=False, stop=True),
                         [P["M1"][h][par], P["Us"]], [bO], inc=False)
                for h in range(2):
                    k.op(k.pe, lambda: nc.tensor.matmul(bS[hr(h), 384:448], P["tokm"][:, c, 0, hr(h)], P["tokm"][:, c, 2, hr(h)], start=True, stop=False),
                         [P["tokm"]], [bS], inc=False)
                    k.op(k.pe, lambda: nc.tensor.matmul(bS[hr(h), 384:448], P["tokm"][:, c, 1, hr(h)], P["Us"][:, hr(h)], start=False, stop=True),
                         [P["tokm"], P["Us"]], [bS], inc=(h == 1))
            yield
            for q, P in enumerate(Ps):
                bS = ps[6 + q]
                gCc = P["gC"][:, c:c + 1]
                k.op(k.dve, lambda: nc.vector.scalar_tensor_tensor(out=P["Hb"][:], in0=bS[:, 384:448], scalar=gCc, in1=P["Hg"][:],
                                                                   op0=ALU.mult, op1=ALU.add), [bS, P["gC"], P["Hg"]], [P["Hb"]])
                k.op(k.dve, lambda: nc.vector.scalar_tensor_tensor(out=P["Hf"][:], in0=bS[:, 384:448], scalar=gCc, in1=P["Hg"][:],
                                                                   op0=ALU.mult, op1=ALU.add), [bS, P["gC"], P["Hg"]], [P["Hf"]])
                if c + 1 < NB:
                    k.op(k.pool, lambda: nc.gpsimd.tensor_scalar(P["Hg"][:], P["Hf"][:], P["gC"][:, c + 1:c + 2], 0.0, ALU.mult, ALU.add),
                         [P["Hf"], P["gC"]], [P["Hg"]])
                k.op(k.act, lambda: nc.scalar.activation(out=P["o_tok"][:, c, :], in_=ps[1][:, q * 128:(q + 1) * 128], func=AF.Copy), [ps[1]], [P["o_tok"]])

        def pass4(hp, P):
            c4 = slice(hp * 128, (hp + 1) * 128)
            o_tok, tokm, prod = P["o_tok"], P["tokm"], P["prod"]
            mk4 = k.mark()
            G0 = k.alloc("G0", [NB, 128], F32)
            G1 = k.alloc("G1", [NB, 128], F32)
            st = k.alloc("st", [4, 32], F32)
            o3 = lambda b: b[:].rearrange("p c (h d) -> p (c h) d", d=64)
            bc = lambda ap: ap.unsqueeze(2).to_broadcast([128, 32, 64])
            k.op(k.dve, lambda: nc.vector.tensor_reduce(out=st[:, 0, :], in_=o3(o_tok), axis=AX.X, op=ALU.add), [o_tok], [st])
            k.op(k.act, lambda: nc.scalar.activation(out=G0[:], in_=o_tok[:], func=AF.Square), [o_tok], [G0])
            k.op(k.dve, lambda: nc.vector.tensor_reduce(out=st[:, 1, :], in_=o3(G0), axis=AX.X, op=ALU.add), [G0], [st])
            k.op(k.dve, lambda: nc.vector.tensor_scalar(st[:, 0, :], st[:, 0, :], -1.0 / 64, None, ALU.mult), [st], [st])
            k.op(k.dve, lambda: nc.vector.tensor_tensor(out=st[:, 2, :], in0=st[:, 0, :], in1=st[:, 0, :], op=ALU.mult), [st], [st])
            k.op(k.dve, lambda: nc.vector.scalar_tensor_tensor(out=st[:, 1, :], in0=st[:, 1, :], scalar=1.0 / 64, in1=st[:, 2, :],
                                                               op0=ALU.mult, op1=ALU.subtract), [st], [st])
            k.op(k.dve, lambda: nc.vector.tensor_scalar(st[:, 1, :], st[:, 1, :], 64e-5, None, ALU.add), [st], [st])
            k.op(k.act, lambda: nc.scalar.activation(out=st[:, 1, :], in_=st[:, 1, :], func=AF.Sqrt), [st], [st])
            k.op(k.dve, lambda: nc.vector.reciprocal(st[:, 1, :], st[:, 1, :]), [st], [st])
            k.op(k.dve, lambda: nc.vector.tensor_tensor(out=o3(G0), in0=o3(o_tok), in1=bc(st[:, 0, :]), op=ALU.add), [o_tok, st], [G0])
            k.op(k.dve, lambda: nc.vector.tensor_tensor(out=o3(G0), in0=o3(G0), in1=bc(st[:, 1, :]), op=ALU.mult), [G0, st], [G0])
            lg = lnx[:, 0, c4].unsqueeze(1).to_broadcast([128, NB, 128])
            lb = lnx[:, 1, c4].unsqueeze(1).to_broadcast([128, NB, 128])
            k.op(k.dve, lambda: nc.vector.tensor_tensor(out=G0[:], in0=G0[:], in1=lg, op=ALU.mult), [G0, lnx], [G0])
            k.op(k.dve, lambda: nc.vector.tensor_tensor(out=G0[:], in0=G0[:], in1=lb, op=ALU.add), [G0, lnx], [G0])
            bk = self.bank()
            for c in range(NB):
                k.op(k.pe, lambda: nc.tensor.matmul(bk[:, 2 * c:2 * c + 2], prod[:, c * 128:(c + 1) * 128], ind2[:], start=True, stop=True),
                     [prod, ind2], [bk], inc=(c == NB - 1))
            k.op(k.act, lambda: nc.scalar.activation(out=st[:, 3, :], in_=bk[:, 0:32], func=AF.Copy), [bk], [st])
            k.op(k.dve, lambda: nc.vector.tensor_tensor(
                out=G1[:].rearrange("p c (h d) -> p c h d", d=64), in0=tokm[:, :, 2, :].rearrange("p c (h d) -> p c h d", d=64),
                in1=st[:, 3, :].rearrange("p (c h) -> p c h", h=2).unsqueeze(3).to_broadcast([128, NB, 2, 64]), op=ALU.mult),
                [tokm, st], [G1])
            k.op(k.dve, lambda: nc.vector.tensor_tensor(out=G0[:], in0=G0[:], in1=G1[:], op=ALU.add), [G0, G1], [G0])
            for g4 in range(4):
                bk = self.bank()
                for i in range(4):
                    c = g4 * 4 + i
                    k.op(k.pe, lambda: nc.tensor.matmul(bk[:, i * 128:(i + 1) * 128], L1T[:, c * 128:(c + 1) * 128], Wg[:, c4], start=True, stop=True),
                         [L1T, Wg], [bk], inc=(i == 3))
                k.op(k.dve, lambda: nc.vector.tensor_tensor(out=self.R_tok[:, g4 * 4:(g4 + 1) * 4, c4],
                                                            in0=G0[:, g4 * 4:(g4 + 1) * 4, :],
                                                            in1=bk[:, :].rearrange("p (c n) -> p c n", n=128), op=ALU.mult),
                     [G0, bk], [self.R_tok])
            k.release(mk4)

        NPAIR = 2
        for grp in range(4 // NPAIR):
            Ps = []
            for q in range(NPAIR):
                P = {}
                P["ar"] = k.alloc(f"ar{q}", [2, S], BF16)
                for nm in ("kT", "bT", "prod", "vs"):
                    P[nm] = k.alloc(f"{nm}{q}", [S], BF16)
                P["gC"] = k.alloc(f"gC{q}", [NB], F32)
                P["Hf"] = k.alloc(f"Hf{q}", [64], F32)
                P["Hg"] = k.alloc(f"Hg{q}", [64], F32)
                P["Hb"] = k.alloc(f"Hb{q}", [64], BF16)
                P["M1"] = [[k.alloc(f"M1_{q}{h}{p}", [2, 128], BF16) for p in range(2)] for h in range(2)]
                P["M2"] = [[k.alloc(f"M2_{q}{h}{p}", [2, 128], BF16) for p in range(2)] for h in range(2)]
                P["Xb"] = [[k.alloc(f"X_{q}{h}{p}", [128], BF16) for p in range(2)] for h in range(2)]
                P["YT"] = [[k.alloc(f"YT_{q}{h}{p}", [2, 128], BF16) for p in range(3)] for h in range(2)]
                P["TTf"] = [[k.alloc(f"TTf_{q}{h}{p}", [128], BF16) for p in range(2)] for h in range(2)]
                P["Zs"] = k.alloc(f"Zs{q}", [128], BF16)
                P["Us"] = k.alloc(f"Us{q}", [128], BF16)
                Ps.append(P)
            mk_f = k.mark()
            wr_, Fb_, FH_ = prep_bufs()
            for q, P in enumerate(Ps):
                prep(grp * NPAIR + q, P, wr_, Fb_, FH_)
            for f_, fh_ in zip(Fb_, FH_):
                k.adopt(f_, fh_)
            k.release(mk_f)
            if grp == 4 // NPAIR - 1:
                self.Wo = Buf(self.hT.ap.rearrange("p a b -> p (a b)")[:, 0:8 * D].rearrange("p (a b) -> p a b", b=D), "Wo")
                sem_wo = k.dma_sem("wout")
                wo_src = self.din["w_out"].rearrange("(kc p) n -> p kc n", p=128)
                for h4 in range(2):
                    k.dma(k.pool, self.Wo[:, h4 * 4:(h4 + 1) * 4, :], wo_src[:, h4 * 4:(h4 + 1) * 4, :], writes=[self.Wo, self.hT], sem=sem_wo)
            for q, P in enumerate(Ps):
                P["tokm"] = k.alloc(f"tokm{q}", [NB, 3, 128], BF16)
                P["o_tok"] = k.alloc(f"o_tok{q}", [NB, 128], F32)
                transposes(P)
                k.op(k.pool, lambda: nc.gpsimd.memset(P["Hf"][:], 0.0), [], [P["Hf"]])
                k.op(k.pool, lambda: nc.gpsimd.memset(P["Hb"][:], 0.0), [], [P["Hb"]])
                k.op(k.pool, lambda: nc.gpsimd.memset(P["Hg"][:], 0.0), [], [P["Hg"]])
            fstate["first"] = True
            for _ in pre(0, Ps):
                pass
            for c in range(NB):
                gens = [seq(c, Ps)] + ([pre(c + 1, Ps)] if c + 1 < NB else [])
                sched = int(os.environ.get("K_SCHED", "1"))
                if sched == 0:
                    for g_ in reversed(gens):
                        for _ in g_:
                            pass
                else:
                    while gens:
                        for g_ in list(gens):
                            try:
                                next(g_)
                            except StopIteration:
                                gens.remove(g_)
            k.op(k.pe, lambda: nc.tensor.matmul(ps[1][:, 0:128], self.identb[:], self.identb[:], start=True, stop=True), [self.identb], [ps[1]])
            for q, P in enumerate(Ps):
                pass4(grp * NPAIR + q, P)
            k.release(mk_hp)

        if "R_tok" in self.debug:
            self.dump("R_tok", self.R_tok, [NB, 512])
        k.release(self.mk_p3c)

    def rstd_from_ss(self, ss2, rstd):
        k, nc = self.k, self.nc
        k.op(k.dve, lambda: nc.vector.tensor_tensor(out=rstd[:], in0=ss2[:, 0:1], in1=ss2[:, 1:2], op=ALU.add), [ss2], [rstd])
        k.op(k.dve, lambda: nc.vector.tensor_scalar(rstd[:], rstd[:], 1.0 / D, 1e-6, ALU.mult, ALU.add), [rstd], [rstd])
        k.op(k.act, lambda: nc.scalar.activation(out=rstd[:], in_=rstd[:], func=AF.Sqrt), [rstd], [rstd])
        k.op(k.dve, lambda: nc.vector.reciprocal(rstd[:], rstd[:]), [rstd], [rstd])

    def phase5(self):
        k, nc = self.k, self.nc
        self.x1 = [k.alloc(f"x1_{t}", [D], F32) for t in range(NB)]
        self.h2T = k.alloc("h2T", [8, 1024], BF16)
        self.n6 = dict(xn=[k.alloc(f"xn6_{i}", [D], F32) for i in range(2)], sq=k.alloc("sq6", [D], BF16),
                       ss=k.alloc("ss6", [8], F32), rs=k.alloc("rs6", [8], F32))
        self.ffn_prefetch()
        mk = k.mark()
        Wo = self.Wo
        gpost = self.load_const("g_post_b", [D])
        catT = [k.alloc(f"catT{i}", [8, 128], BF16) for i in range(3)]
        tmp = [k.alloc(f"tmp{i}", [D], F32) for i in range(2)]
        sq = k.alloc("sq5", [512], F32)
        ss2 = [k.alloc(f"ss2_{i}", [2], F32) for i in range(2)]
        rstd = [k.alloc(f"rstd5_{i}", [1], F32) for i in range(2)]
        xr = [k.alloc(f"xr{i}", [D], F32) for i in range(3)]

        def stage_a(n):
            xt = xr[n % 3]
            k.dma(k.sp, xt[:], self.x[n * 128:(n + 1) * 128, :], writes=[xt], sem=self.x_sems[n % 3])
            ct = catT[n % 3]
            bk = self.bank()
            pb = bk.ap.bitcast(BF16)
            for kc in range(8):
                src = self.A_tok if kc < 4 else self.R_tok
                cc = (kc % 4) * 128
                k.op(k.pe, lambda: nc.tensor.transpose(out=pb[:, kc * 128:(kc + 1) * 128], in_=src[:, n, cc:cc + 128], identity=self.identb[:]),
                     [src, self.identb], [bk], inc=(kc == 7))
            copy_on(k, k.act if os.environ.get("K_P5A", "1") == "1" else k.ev(), ct[:].rearrange("p a b -> p (a b)"), pb[:, 0:1024], [bk], [ct])

        def stage_b(n):
            xt = xr[n % 3]
            ct = catT[n % 3]
            banks = []
            for dh in range(2):
                bm = self.bank()
                for kc in range(8):
                    k.op(k.pe, lambda: nc.tensor.matmul(bm[:, :], ct[:, kc, :], Wo[:, kc, dh * 512:(dh + 1) * 512], start=(kc == 0), stop=(kc == 7)),
                         [ct, Wo], [bm], inc=(kc == 7))
                k.op(k.act, lambda: nc.scalar.activation(out=sq[:], in_=bm[:, :], func=AF.Square, accum_out=ss2[n % 2][:, dh:dh + 1]),
                     [bm], [sq, ss2[n % 2]])
                banks.append(bm)
            self.rstd_from_ss(ss2[n % 2], rstd[n % 2])
            tm = tmp[n % 2]
            for dh in range(2):
                k.op(k.dve, lambda: nc.vector.scalar_tensor_tensor(out=tm[:, dh * 512:(dh + 1) * 512], in0=banks[dh][:, :], scalar=rstd[n % 2][:, 0:1],
                                                                   in1=gpost[:, dh * 512:(dh + 1) * 512], op0=ALU.mult, op1=ALU.mult),
                     [banks[dh], rstd[n % 2], gpost], [tm])
            k.op(k.dve, lambda: nc.vector.tensor_tensor(out=self.x1[n][:], in0=tm[:], in1=xt[:], op=ALU.add), [tm, xt], [self.x1[n]])

        stage_a(0)
        hgen = None
        for n in range(NB):
            if n + 1 < NB:
                stage_a(n + 1)
            stage_b(n)
            if n == 8:
                hgen = self.h2_norm(0)
            if hgen is not None:
                next(hgen, None)
        for _ in hgen:
            pass
        if "x1" in self.debug:
            o = self.dbg_out("x1", [S, D])
            for n in range(NB):
                k.dma(k.sp, o[n * 128:(n + 1) * 128, :], self.x1[n][:], reads=[self.x1[n]], sem=self.sem_dbg)
        k.release(mk)

    def h2_norm(self, hf):
        n6 = self.n6
        akc = tuple(range(8)) if os.environ.get("K_H2A", "1") == "1" else None
        return self.norm_T_gen(lambda t, xt: None, self.g_ffn, self.h2T, 4, self.x1[hf * 8:(hf + 1) * 8], n6["xn"], n6["sq"], n6["ss"], n6["rs"],
                               act_kc=akc)

    def ffn_prefetch(self):
        k, nc = self.k, self.nc
        F_ = self.F6 = {}
        F_["cw"] = self.load_const("conv_w", [64, 3])
        F_["cb"] = self.load_const("conv_b", [64])
        F_["gfp"] = self.load_const("g_fpost_b", [D])
        F_["halo"] = k.alloc("halo", [64, 2], F32)
        k.op(k.pool, lambda: nc.gpsimd.memset(F_["halo"][:], 0.0), [], [F_["halo"]])
        F_["hl_t"] = [k.alloc(f"hl_t{i}", [2], F32) for i in range(2)]
        NUP = 3
        F_["sem_up"] = [k.dma_sem(f"wup{i}") for i in range(NUP)]
        F_["sem_dn"] = [k.dma_sem(f"wdn{i}") for i in range(2)]
        F_["wup"] = [k.alloc(f"wup{i}", [8, 256], BF16) for i in range(NUP)]
        self.load_up(0)
        self.load_up(1)

    def load_up(self, ii):
        k = self.k
        F_ = self.F6
        up_src = self.din["w_up"].rearrange("(kc p) n -> p kc n", p=128)
        i = ii % 32
        wu = F_["wup"][ii % 3]
        for h4 in range(2):
            k.dma(k.pool, wu[:, h4 * 4:(h4 + 1) * 4, :], up_src[:, h4 * 4:(h4 + 1) * 4, i * 256:(i + 1) * 256], writes=[wu], sem=F_["sem_up"][ii % 3])

    def load_dn(self, gg):
        k = self.k
        F_ = self.F6
        dn_src = self.din["w_down"].rearrange("(c p) n -> p c n", p=128)
        g = gg % 4
        w = F_["wd"][gg % 2]
        for h4 in range(2):
            k.dma(k.pool, w[:, h4 * 4:(h4 + 1) * 4, :], dn_src[:, g * 8 + h4 * 4:g * 8 + (h4 + 1) * 4, :], writes=[w], sem=F_["sem_dn"][gg % 2])

    def phase6(self):
        k, nc = self.k, self.nc
        k.adopt(self.hT, [self.Wo])
        k.release_high(k.cap)
        mk = k.mark()
        F_ = self.F6
        F_["wd"] = [k.alloc(f"wd{i}", [8, D], BF16) for i in range(2)]
        self.load_dn(0)
        cw, cb, gfp, halo, hl_t, wup, wd = F_["cw"], F_["cb"], F_["gfp"], F_["halo"], F_["hl_t"], F_["wup"], F_["wd"]
        NUP = 3
        load_up, load_dn = self.load_up, self.load_dn
        NGT = 9
        h2T = self.h2T
        GT = [k.alloc(f"GT{c}", [1024], BF16) for c in range(NGT)]
        f = [k.alloc(f"f{t}", [D], F32) for t in range(8)]
        Cb = [k.alloc(f"Cb{i}", [512], F32) for i in range(5)]
        Gg = [k.alloc(f"Gg{i}", [512], F32) for i in range(2)]
        ssf = [k.alloc(f"ssf{i}", [1], F32) for i in range(2)]
        st = {"ui": 0}

        def down(hf, g):
            w = wd[(hf * 4 + g) % 2]
            for tt in range(8):
                for dh in range(2):
                    bk = self.bank()
                    for ci in range(8):
                        gt = GT[(hf * 32 + g * 8 + ci) % NGT]
                        k.op(k.pe, lambda: nc.tensor.matmul(bk[:, :], gt[:, tt * 128:(tt + 1) * 128], w[:, ci, dh * 512:(dh + 1) * 512],
                                                            start=(ci == 0), stop=(ci == 7)), [gt, w], [bk], inc=(ci == 7))
                    fs = f[tt][:, dh * 512:(dh + 1) * 512]
                    if g == 0:
                        k.op(k.act, lambda: nc.scalar.activation(out=fs, in_=bk[:, :], func=AF.Copy), [bk], [f[tt]])
                    else:
                        k.op(k.dve, lambda: nc.vector.tensor_tensor(out=fs, in0=bk[:, :], in1=fs, op=ALU.add), [bk, f[tt]], [f[tt]])
                yield tt

        def final(hf, tts=range(8)):
            for tt in tts:
                n = hf * 8 + tt
                s1 = ssf[tt % 2]
                sqj = self.n6["sq"]
                k.op(k.act, lambda: nc.scalar.activation(out=sqj[:], in_=f[tt][:], func=AF.Square, accum_out=s1[:, 0:1]), [f[tt]], [sqj, s1])
                k.op(k.dve, lambda: nc.vector.tensor_scalar(s1[:], s1[:], 1.0 / D, 1e-6, ALU.mult, ALU.add), [s1], [s1])
                k.op(k.act, lambda: nc.scalar.activation(out=s1[:], in_=s1[:], func=AF.Sqrt), [s1], [s1])
                k.op(k.dve, lambda: nc.vector.reciprocal(s1[:], s1[:]), [s1], [s1])
                o_ = self.n6["xn"][tt % 2]
                k.op(k.dve, lambda: nc.vector.scalar_tensor_tensor(out=o_[:], in0=f[tt][:], scalar=s1[:, 0:1], in1=gfp[:], op0=ALU.mult, op1=ALU.mult),
                     [f[tt], s1, gfp], [o_])
                k.op(k.dve, lambda: nc.vector.tensor_tensor(out=o_[:], in0=o_[:], in1=self.x1[n][:], op=ALU.add), [o_, self.x1[n]], [o_])
                k.dma(k.sp, self.out[n * 128:(n + 1) * 128, :], o_[:], reads=[o_], sem=self.sem_o2[tt % 2])

        def up(hf, i):
            g, ci = divmod(i, 8)
            ii = hf * 32 + i
            if ci == 0 and ii > 0:
                load_dn(hf * 4 + g)
            if ii + 2 < 64:
                load_up(ii + 2)
            wu = wup[ii % NUP]
            gtb = GT[ii % NGT]
            prev_bk = {}
            for tq in range(2):
                res = {}
                for gv in range(2):
                    c2 = 2 * i + gv
                    bk = self.bank()
                    for kc in range(8):
                        k.op(k.pe, lambda: nc.tensor.matmul(bk[:, :], wu[:, kc, gv * 128:(gv + 1) * 128], h2T[:, kc, tq * 512:(tq + 1) * 512],
                                                            start=(kc == 0), stop=(kc == 7)), [wu, h2T], [bk], inc=(kc == 7))
                    c = Cb[st["ui"] % 5]
                    st["ui"] += 1
                    w0, w1, w2 = cw[:, c2, 0:1], cw[:, c2, 1:2], cw[:, c2, 2:3]
                    k.op(k.act, lambda: nc.scalar.activation(out=c[:], in_=bk[:, :], func=AF.Identity, scale=w2, bias=cb[:, c2:c2 + 1]),
                         [bk, cw, cb], [c])
                    k.op(k.dve, lambda: nc.vector.scalar_tensor_tensor(out=c[:, 1:512], in0=bk[:, 0:511], scalar=w1, in1=c[:, 1:512],
                                                                       op0=ALU.mult, op1=ALU.add), [bk, cw, c], [c])
                    k.op(k.dve, lambda: nc.vector.scalar_tensor_tensor(out=c[:, 2:512], in0=bk[:, 0:510], scalar=w0, in1=c[:, 2:512],
                                                                       op0=ALU.mult, op1=ALU.add), [bk, cw, c], [c])
                    if tq == 0:
                        hsrc, hb = (halo[:, c2, :], halo) if hf == 1 else (None, None)
                        prev_bk[gv] = bk
                    else:
                        hsrc, hb = prev_bk[gv][:, 510:512], prev_bk[gv]
                    if hsrc is not None:
                        k.op(k.dve, lambda: nc.vector.scalar_tensor_tensor(out=c[:, 0:2], in0=hsrc, scalar=w0, in1=c[:, 0:2],
                                                                           op0=ALU.mult, op1=ALU.add), [hb, cw, c], [c])
                        k.op(k.dve, lambda: nc.vector.scalar_tensor_tensor(out=c[:, 0:1], in0=hsrc[:, 1:2], scalar=w1, in1=c[:, 0:1],
                                                                           op0=ALU.mult, op1=ALU.add), [hb, cw, c], [c])
                    if tq == 1 and hf == 0:
                        k.op(k.dve, lambda: nc.vector.tensor_copy(halo[:, c2, :], bk[:, 510:512]), [bk], [halo])
                    res[gv] = c
                gg = Gg[(i * 2 + tq) % 2]
                k.op(k.act, lambda: nc.scalar.activation(out=gg[:], in_=res[0][:], func=AF.Gelu_apprx_tanh), [res[0]], [gg])
                k.op(k.pool, lambda: nc.gpsimd.tensor_tensor(out=gtb[:, tq * 512:(tq + 1) * 512], in0=gg[:], in1=res[1][:], op=ALU.mult),
                     [gg, res[1]], [gtb])

        for hf in range(2):
            for i in range(32):
                up(hf, i)
                if i % 8 == 0 and i > 0:
                    for _ in down(hf, i // 8 - 1):
                        pass
            if hf == 0:
                hg = self.h2_norm(1)
                dg = down(0, 3)
                next(hg, None)
                alive = True
                while alive:
                    alive = False
                    for _ in range(2):
                        if next(dg, None) is not None:
                            alive = True
                    for _ in range(2):
                        if next(hg, None) is not None:
                            alive = True
                final(0)
        for tt_done in down(1, 3):
            final(1, [tt_done])
        k.release(mk)


_CACHE = {}


def kernel(**inputs):
    sh = host_layout(inputs)
    x = np.asarray(inputs["x"], dtype=np.float32)
    B = x.shape[0]
    if "prog" not in _CACHE:
        p = Prog()
        p.build()
        _CACHE["prog"] = p
    p = _CACHE["prog"]
    in_maps = [dict(sh, x=np.ascontiguousarray(x[b])) for b in range(B)]
    res = run_bass_kernel_spmd(p.nc, in_maps, core_ids=list(range(B)))
    return np.stack([np.asarray(r["out"], dtype=np.float32) for r in res.results], axis=0)
```
